# Optimizing a Trainium2 kernel written in Bass

```python
import math
import jax
import jax.numpy as jnp
from jax import lax
import numpy as np

D_MODEL = 4096
BATCH = 2
SEQ = 8192
DEPTH = 1

CTX_LEN = 256
GRID_W = 64
HEAD_DIM = 128
N_HEADS = 16
N_KV_HEADS = 4
GQA_GROUP = N_HEADS // N_KV_HEADS
WINDOW = 128
BLOCK = 128
ROPE_BASE = 10000.0
ROPE_PAIRS = HEAD_DIM // 4
SSM_HEAD_DIM = 64
SSM_INNER = D_MODEL // 2
SSM_HEADS = SSM_INNER // SSM_HEAD_DIM
SSM_GROUPS = 4
SSM_HEADS_PER_GROUP = SSM_HEADS // SSM_GROUPS
SSM_STATE = 128
CONV_K = 5
CHUNK = 128
ATTN_WIDTH = N_HEADS * HEAD_DIM
MIX_WIDTH = ATTN_WIDTH + SSM_INNER
KV_COLS = N_KV_HEADS * HEAD_DIM
BC_COLS = SSM_GROUPS * SSM_STATE
XBC_COLS = SSM_INNER + 2 * BC_COLS
DT_COLS = 2 * SSM_HEADS
CTX_COL0 = ATTN_WIDTH + SSM_INNER
IN_COLS = CTX_COL0 + 2 * KV_COLS + XBC_COLS + DT_COLS
D_FF = 128 * ((8 * D_MODEL // 3 + 127) // 128)
N_MOD = 9
EPS = 1e-6
NEG_INF = -1e30

kernel_name = 'hymba_swa_ssd_macaron_dit_block'


def rmsnorm(t, g):
    tf = t.astype(jnp.float32)
    return (tf * lax.rsqrt(jnp.mean(tf * tf, axis=-1, keepdims=True) + EPS) * g).astype(t.dtype)


def modulate(t, shift, scale):
    return t * (1 + scale[:, None]) + shift[:, None]


def add_residual(s, y, g_post, gate, coef):
    return s + coef * gate[:, None] * rmsnorm(y, g_post)


def adaln(cv, w_mod, b_mod):
    return (jax.nn.silu(cv) @ w_mod + b_mod).reshape(cv.shape[0], N_MOD, D_MODEL)


def swiglu(h, wg, wu, wd):
    return (jax.nn.silu(h @ wg) * (h @ wu)) @ wd


def ffn_half(s, mod, slot, g_pre, g_post, wg, wu, wd):
    h = modulate(rmsnorm(s, g_pre), mod[:, 3 * slot], mod[:, 3 * slot + 1])
    return add_residual(s, swiglu(h, wg, wu, wd), g_post, mod[:, 3 * slot + 2], 0.5)


def rope_tables(rows_count):
    row = jnp.repeat(jnp.arange(rows_count), GRID_W).astype(jnp.float32)
    col = jnp.tile(jnp.arange(GRID_W), rows_count).astype(jnp.float32)
    inv = ROPE_BASE ** (-jnp.arange(ROPE_PAIRS, dtype=jnp.float32) / ROPE_PAIRS)
    ar = row[:, None] * inv
    ac = col[:, None] * inv
    ang = jnp.concatenate([ar, ar, ac, ac], axis=-1)
    return jnp.cos(ang), jnp.sin(ang)


def apply_rope(t, cos, sin):
    tf = t.astype(jnp.float32)
    seg = tf.reshape(*tf.shape[:-1], 2, 2, ROPE_PAIRS)
    rot = jnp.concatenate([-seg[..., 1:, :], seg[..., :1, :]], axis=-2).reshape(tf.shape)
    return (tf * cos[None, :, None] + rot * sin[None, :, None]).astype(t.dtype)


def sink_softmax(scores, sink):
    m = sink
    for s in scores:
        m = jnp.maximum(m, s.max(axis=-1, keepdims=True))
    ps = [jnp.exp(s - m) for s in scores]
    denom = jnp.exp(sink - m) + sum(p.sum(axis=-1, keepdims=True) for p in ps)
    return [p / denom for p in ps]


def window_attention(q, k, v, k_ctx, v_ctx, sink):
    bsz, n = q.shape[:2]
    nb = n // BLOCK
    scale = HEAD_DIM ** -0.5
    qb = q.reshape(bsz, nb, BLOCK, N_KV_HEADS, GQA_GROUP, HEAD_DIM)

    def band(t):
        tp = jnp.pad(t, ((0, 0), (BLOCK, BLOCK), (0, 0), (0, 0)))
        tp = tp.reshape(bsz, nb + 2, BLOCK, N_KV_HEADS, HEAD_DIM)
        return jnp.concatenate([tp[:, :-2], tp[:, 1:-1], tp[:, 2:]], axis=2)

    kb, vb = band(k), band(v)
    s_band = jnp.einsum('bnqkgd,bnskd->bnkgqs', qb, kb, preferred_element_type=jnp.float32) * scale
    s_ctx = jnp.einsum('bnqkgd,bskd->bnkgqs', qb, k_ctx, preferred_element_type=jnp.float32) * scale
    qi = jnp.arange(BLOCK)[:, None]
    kj = jnp.arange(3 * BLOCK)[None, :]
    in_window = jnp.abs(kj - BLOCK - qi) <= WINDOW
    key_pos = (jnp.arange(nb)[:, None] - 1) * BLOCK + jnp.arange(3 * BLOCK)[None, :]
    in_range = (key_pos >= 0) & (key_pos < n)
    mask = in_window[None] & in_range[:, None, :]
    s_band = jnp.where(mask[None, :, None, None], s_band, NEG_INF)
    sink_l = sink.astype(jnp.float32).reshape(1, 1, N_KV_HEADS, GQA_GROUP, 1, 1)
    p_band, p_ctx = sink_softmax([s_band, s_ctx], sink_l)
    o = (jnp.einsum('bnkgqs,bnskd->bnqkgd', p_band.astype(v.dtype), vb)
         + jnp.einsum('bnkgqs,bskd->bnqkgd', p_ctx.astype(v.dtype), v_ctx))
    return o.reshape(bsz, n, ATTN_WIDTH)


def context_attention(q, k, v, sink):
    bsz, lc = q.shape[:2]
    qg = q.reshape(bsz, lc, N_KV_HEADS, GQA_GROUP, HEAD_DIM)
    s = jnp.einsum('bqkgd,bskd->bkgqs', qg, k, preferred_element_type=jnp.float32) * HEAD_DIM ** -0.5
    (p,) = sink_softmax([s], sink.astype(jnp.float32).reshape(1, N_KV_HEADS, GQA_GROUP, 1, 1))
    o = jnp.einsum('bkgqs,bskd->bqkgd', p.astype(v.dtype), v)
    return o.reshape(bsz, lc, ATTN_WIDTH)


def depthwise_conv(u, w, b):
    out = lax.conv_general_dilated(
        u, w[:, None, :].astype(u.dtype), window_strides=(1,),
        padding=[(CONV_K // 2, CONV_K // 2)], dimension_numbers=('NWC', 'WIO', 'NWC'),
        feature_group_count=u.shape[-1])
    return out + b


def split_tail(p):
    bsz, L = p.shape[:2]
    k = p[..., :KV_COLS].reshape(bsz, L, N_KV_HEADS, HEAD_DIM)
    v = p[..., KV_COLS:2 * KV_COLS].reshape(bsz, L, N_KV_HEADS, HEAD_DIM)
    xbc = p[..., 2 * KV_COLS:2 * KV_COLS + XBC_COLS]
    dt_raw = p[..., 2 * KV_COLS + XBC_COLS:]
    return k, v, xbc, dt_raw


def split_ssm(xbc, dt_raw, dt_bias):
    bsz, L = xbc.shape[:2]
    xs = xbc[..., :SSM_INNER].reshape(bsz, L, SSM_GROUPS, SSM_HEADS_PER_GROUP, SSM_HEAD_DIM)
    bs = xbc[..., SSM_INNER:SSM_INNER + BC_COLS].reshape(bsz, L, SSM_GROUPS, SSM_STATE)
    cs = xbc[..., SSM_INNER + BC_COLS:].reshape(bsz, L, SSM_GROUPS, SSM_STATE)
    dt = jax.nn.softplus(
        dt_raw.astype(jnp.float32).reshape(bsz, L, 2, SSM_GROUPS, SSM_HEADS_PER_GROUP)
        + dt_bias.astype(jnp.float32).reshape(2, SSM_GROUPS, SSM_HEADS_PER_GROUP))
    return xs, bs, cs, dt[:, :, 0], dt[:, :, 1]


def ssd_scan(xs, dt, a_log, bs, cs, h0, with_y):
    bsz, n = xs.shape[:2]
    nc = n // CHUNK
    a = -jnp.exp(a_log.astype(jnp.float32))
    acs = jnp.cumsum((dt * a).reshape(bsz, nc, CHUNK, SSM_GROUPS, SSM_HEADS_PER_GROUP), axis=2)
    a_tot = acs[:, :, -1]
    xdt = (xs.astype(jnp.float32) * dt[..., None]).reshape(
        bsz, nc, CHUNK, SSM_GROUPS, SSM_HEADS_PER_GROUP, SSM_HEAD_DIM)
    bch = bs.astype(jnp.float32).reshape(bsz, nc, CHUNK, SSM_GROUPS, SSM_STATE)
    states = jnp.einsum('bcqgn,bcqgr,bcqgrp->bcgrpn', bch, jnp.exp(a_tot[:, :, None] - acs), xdt)

    def step(h, inp):
        decay, st = inp
        return decay[..., None, None] * h + st, (h if with_y else None)

    h_final, h_prev = lax.scan(step, h0, (jnp.moveaxis(jnp.exp(a_tot), 1, 0), jnp.moveaxis(states, 1, 0)))
    if not with_y:
        return None, h_final
    h_prev = jnp.moveaxis(h_prev, 0, 1)
    cch = cs.astype(jnp.float32).reshape(bsz, nc, CHUNK, SSM_GROUPS, SSM_STATE)
    y_off = jnp.einsum('bcqgn,bcgrpn->bcqgrp', cch, h_prev) * jnp.exp(acs)[..., None]
    tril = jnp.tril(jnp.ones((CHUNK, CHUNK), dtype=bool))
    seg = acs[:, :, :, None] - acs[:, :, None]
    decay_ij = jnp.exp(jnp.where(tril[:, :, None, None], seg, -jnp.inf))
    cb = jnp.einsum('bcign,bcjgn->bcijg', cch, bch)
    y_diag = jnp.einsum('bcijgr,bcjgrp->bcigrp', cb[..., None] * decay_ij, xdt)
    return (y_diag + y_off).reshape(bsz, n, SSM_GROUPS, SSM_HEADS_PER_GROUP, SSM_HEAD_DIM), h_final


def flip(t):
    return jnp.flip(t, axis=1)


def bidirectional_ssd(lat, ctx, a_log, d_skip, with_ctx_y):
    xs, bs, cs, dtf, dtb = lat
    xc, bc, cc, dtcf, dtcb = ctx
    bsz = xs.shape[0]
    h0 = jnp.zeros((bsz, SSM_GROUPS, SSM_HEADS_PER_GROUP, SSM_HEAD_DIM, SSM_STATE), jnp.float32)
    a_f = a_log[0].reshape(SSM_GROUPS, SSM_HEADS_PER_GROUP)
    a_b = a_log[1].reshape(SSM_GROUPS, SSM_HEADS_PER_GROUP)
    yc_f, hc_f = ssd_scan(xc, dtcf, a_f, bc, cc, h0, with_ctx_y)
    yc_b, hc_b = ssd_scan(flip(xc), flip(dtcb), a_b, flip(bc), flip(cc), h0, with_ctx_y)
    y_f, _ = ssd_scan(xs, dtf, a_f, bs, cs, hc_f, True)
    y_b, _ = ssd_scan(flip(xs), flip(dtb), a_b, flip(bs), flip(cs), hc_b, True)
    d = d_skip.astype(jnp.float32).reshape(SSM_GROUPS, SSM_HEADS_PER_GROUP, 1)
    y_lat = (y_f + flip(y_b) + d * xs).reshape(bsz, xs.shape[1], SSM_INNER)
    if not with_ctx_y:
        return y_lat, None
    y_ctx = (yc_f + flip(yc_b) + d * xc).reshape(bsz, xc.shape[1], SSM_INNER)
    return y_lat, y_ctx


def merge_groups(attn, y_ssm, z, attn_norm, ssm_norm, w_out):
    a = rmsnorm(attn, attn_norm)
    s = rmsnorm(y_ssm.astype(z.dtype) * jax.nn.silu(z), ssm_norm)
    return jnp.concatenate([a, s], axis=-1) @ w_out


def hybrid_mix(h_lat, h_ctx, cos, sin, w_in, sink, attn_norm, conv_w, conv_b, a_log, dt_bias,
               d_skip, ssm_norm, w_out, with_ctx_out):
    bsz, n = h_lat.shape[:2]
    lc = h_ctx.shape[1]
    p = h_lat @ w_in
    q = apply_rope(p[..., :ATTN_WIDTH].reshape(bsz, n, N_HEADS, HEAD_DIM), cos, sin)
    z = p[..., ATTN_WIDTH:CTX_COL0]
    k, v, xbc, dt_raw = split_tail(p[..., CTX_COL0:])
    k = apply_rope(k, cos, sin)
    kc, vc, xbcc, dtc = split_tail(h_ctx @ w_in[:, CTX_COL0:])
    attn = window_attention(q, k, v, kc, vc, sink)
    lat_ssm = split_ssm(jax.nn.silu(depthwise_conv(xbc, conv_w, conv_b)), dt_raw, dt_bias)
    ctx_ssm = split_ssm(jax.nn.silu(depthwise_conv(xbcc, conv_w, conv_b)), dtc, dt_bias)
    y_lat, y_ctx = bidirectional_ssd(lat_ssm, ctx_ssm, a_log, d_skip, with_ctx_out)
    out_lat = merge_groups(attn, y_lat, z, attn_norm, ssm_norm, w_out)
    if not with_ctx_out:
        return out_lat, None
    pc_head = h_ctx @ w_in[:, :CTX_COL0]
    qc = pc_head[..., :ATTN_WIDTH].reshape(bsz, lc, N_HEADS, HEAD_DIM)
    attn_c = context_attention(qc, kc, vc, sink)
    out_ctx = merge_groups(attn_c, y_ctx, pc_head[..., ATTN_WIDTH:], attn_norm, ssm_norm, w_out)
    return out_lat, out_ctx


def setup_inputs(seed: int = 0) -> dict:
    key = jax.random.key(seed)
    ks = jax.random.split(key, 21)
    f32 = jnp.float32

    def normal(k, shape, scale):
        return jax.random.normal(k, shape, f32) * scale

    dt0 = jnp.exp(jax.random.uniform(ks[17], (DEPTH, 2, SSM_HEADS), f32, math.log(1e-3), math.log(1e-1)))
    return {
        'x': normal(ks[0], (BATCH, SEQ, D_MODEL), 1.0),
        'c': normal(ks[1], (BATCH, D_MODEL), 1.0),
        'ctx': normal(ks[2], (BATCH, CTX_LEN, D_MODEL), 1.0),
        'c_ctx': normal(ks[3], (D_MODEL,), 1.0),
        'w_mod': normal(ks[4], (DEPTH, D_MODEL, N_MOD * D_MODEL), 0.5 * D_MODEL ** -0.5),
        'b_mod': normal(ks[5], (DEPTH, N_MOD * D_MODEL), 0.02),
        'norm_pre': 1.0 + normal(ks[6], (DEPTH, 3, D_MODEL), 0.1),
        'norm_post': 1.0 + normal(ks[7], (DEPTH, 3, D_MODEL), 0.1),
        'w_ffn_gate': normal(ks[8], (DEPTH, 2, D_MODEL, D_FF), D_MODEL ** -0.5),
        'w_ffn_up': normal(ks[9], (DEPTH, 2, D_MODEL, D_FF), D_MODEL ** -0.5),
        'w_ffn_down': normal(ks[10], (DEPTH, 2, D_FF, D_MODEL), D_FF ** -0.5),
        'w_in': normal(ks[11], (DEPTH, D_MODEL, IN_COLS), D_MODEL ** -0.5),
        'attn_sink': normal(ks[12], (DEPTH, N_HEADS), 1.0),
        'attn_norm': 1.0 + normal(ks[13], (DEPTH, ATTN_WIDTH), 0.1),
        'conv_w': normal(ks[14], (DEPTH, CONV_K, XBC_COLS), CONV_K ** -0.5),
        'conv_b': normal(ks[15], (DEPTH, XBC_COLS), 0.02),
        'a_log': jnp.log(jax.random.uniform(ks[16], (DEPTH, 2, SSM_HEADS), f32, 1.0, 16.0)),
        'dt_bias': dt0 + jnp.log(-jnp.expm1(-dt0)),
        'd_skip': 1.0 + normal(ks[18], (DEPTH, SSM_HEADS), 0.1),
        'ssm_norm': 1.0 + normal(ks[19], (DEPTH, SSM_INNER), 0.1),
        'w_out': normal(ks[20], (DEPTH, MIX_WIDTH, D_MODEL), MIX_WIDTH ** -0.5),
    }


def reference(x, c, ctx, c_ctx, w_mod, b_mod, norm_pre, norm_post, w_ffn_gate, w_ffn_up, w_ffn_down,
              w_in, attn_sink, attn_norm, conv_w, conv_b, a_log, dt_bias, d_skip, ssm_norm, w_out):
    ROWS = x.shape[1] // GRID_W
    cos, sin = rope_tables(ROWS)
    for layer in range(DEPTH):
        last = layer == DEPTH - 1
        mod_x = adaln(c, w_mod[layer], b_mod[layer])
        mod_c = adaln(c_ctx[None], w_mod[layer], b_mod[layer])
        ffn0 = (norm_pre[layer, 0], norm_post[layer, 0], w_ffn_gate[layer, 0], w_ffn_up[layer, 0], w_ffn_down[layer, 0])
        x = ffn_half(x, mod_x, 0, *ffn0)
        ctx = ffn_half(ctx, mod_c, 0, *ffn0)
        hx = modulate(rmsnorm(x, norm_pre[layer, 1]), mod_x[:, 3], mod_x[:, 4])
        hc = modulate(rmsnorm(ctx, norm_pre[layer, 1]), mod_c[:, 3], mod_c[:, 4])
        y_x, y_c = hybrid_mix(hx, hc, cos, sin, w_in[layer], attn_sink[layer], attn_norm[layer],
                              conv_w[layer], conv_b[layer], a_log[layer], dt_bias[layer], d_skip[layer],
                              ssm_norm[layer], w_out[layer], not last)
        x = add_residual(x, y_x, norm_post[layer, 1], mod_x[:, 5], 1.0)
        ffn1 = (norm_pre[layer, 2], norm_post[layer, 2], w_ffn_gate[layer, 1], w_ffn_up[layer, 1], w_ffn_down[layer, 1])
        x = ffn_half(x, mod_x, 2, *ffn1)
        if not last:
            ctx = add_residual(ctx, y_c, norm_post[layer, 1], mod_c[:, 5], 1.0)
            ctx = ffn_half(ctx, mod_c, 2, *ffn1)
    return x
```

```python
import contextlib
import os
MASK = int(os.environ.get('INPROJ_MASK', '127'))
SSDM = int(os.environ.get('SSD_MODE', '7'))
YSTEPS = int(os.environ.get('YSTEPS', '99'))
GP_LIM = int(os.environ.get('GP_LIM', '4'))
import numpy as np
import concourse.bass as bass
import concourse.mybir as mybir
from concourse.bass_utils import run_bass_kernel_spmd

F32 = mybir.dt.float32
BF16 = mybir.dt.bfloat16
ALU = mybir.AluOpType
AF = mybir.ActivationFunctionType
AX = mybir.AxisListType


class Cfg:
    def __init__(self, D=4096, SEQ=8192, CTX=256, DFF=11008, GRID_W=64):
        self.D = D
        self.KC = D // 128
        self.SEQ = SEQ
        self.CTX = CTX
        self.DFF = DFF
        self.FC = DFF // 128
        self.GRID_W = GRID_W
        self.T = 512
        assert CTX == 256 and SEQ % self.T == 0
        self.NT = SEQ // self.T
        self.NH = 16
        self.NKV = 4
        self.DH = 128
        self.AW = self.NH * self.DH
        self.SI = D // 2
        self.SH = self.SI // 64
        self.SG = 4
        self.HPG = self.SH // 4
        self.NS = 128
        self.KV = self.NKV * self.DH
        self.BC = self.SG * self.NS
        self.XBC = self.SI + 2 * self.BC
        self.DT = 2 * self.SH
        self.C0 = self.AW + self.SI
        self.INC = self.C0 + 2 * self.KV + self.XBC + self.DT
        self.MW = self.AW + self.SI
        self.EPS = 1e-6


class Buf:
    def __init__(self, name):
        self.name = name
        self.last_w = None
        self.readers = {}
        self.dsem = None
        self.dcnt = 0


class Prog:
    def __init__(self, nc, stack):
        self.nc = nc
        self.stack = stack
        self.lists = {k: [] for k in ("sync", "scalar", "gpsimd", "vector", "tensor")}
        self.sems = {}
        self.cnt = {k: 0 for k in self.lists}
        self.waited = {k: {} for k in self.lists}
        for k in self.lists:
            self.sems[k] = stack.enter_context(nc.semaphore("e_" + k))
        self.nsem = 5
        self.store_events = {}
        self.dma_bufs = []
        self.gp_fifo = []

    def new_sem(self, name):
        self.nsem += 1
        return self.stack.enter_context(self.nc.semaphore(name))

    def _wait(self, q, ev):
        if ev is None:
            return
        sem, val = ev
        if sem is self.sems.get(q):
            if q == "tensor" or val <= self.cnt[q] - 8:
                return
        key = id(sem)
        if self.waited[q].get(key, 0) >= val:
            return
        self.waited[q][key] = val
        self.lists[q].append(lambda e, sem=sem, val=val: e.wait_ge(sem, val))

    def _deps(self, q, reads, writes):
        for b in reads:
            self._wait(q, b.last_w)
        for b in writes:
            self._wait(q, b.last_w)
            for sem_val in b.readers.values():
                self._wait(q, sem_val)

    @staticmethod
    def _add_reader(b, ev):
        key = id(ev[0])
        if key not in b.readers or b.readers[key][1] < ev[1]:
            b.readers[key] = ev

    def op(self, q, fn, reads=(), writes=()):
        self._deps(q, reads, writes)
        self.cnt[q] += 1
        sem = self.sems[q]
        self.lists[q].append(lambda e, fn=fn, sem=sem: fn(e).then_inc(sem, 1))
        ev = (sem, self.cnt[q])
        for b in reads:
            self._add_reader(b, ev)
        for b in writes:
            b.last_w = ev
            b.readers = {}
        return ev

    def dma(self, q, outs_ins, reads=(), writes=(), sem_owner=None):
        self._deps(q, reads, writes)
        ow = sem_owner
        if ow.dsem is None:
            ow.dsem = self.new_sem("d_" + ow.name)
            self.dma_bufs.append(ow)
        for n_, (o, i) in enumerate(outs_ins):
            if q == "gpsimd":
                if len(self.gp_fifo) >= GP_LIM:
                    s_, v_ = self.gp_fifo.pop(0)
                    self.lists[q].append(lambda e, s_=s_, v_=v_: e.wait_ge(s_, v_))
            ow.dcnt += 16
            self.lists[q].append(lambda e, o=o, i=i, s=ow.dsem: e.dma_start(out=o, in_=i).then_inc(s, 16))
            if q == "gpsimd":
                self.gp_fifo.append((ow.dsem, ow.dcnt))
        ev = (ow.dsem, ow.dcnt)
        for b in reads:
            self._add_reader(b, ev)
        for b in writes:
            b.last_w = ev
            b.readers = {}
        return ev

    def fence_events(self, q, evs):
        for ev in evs:
            self._wait(q, ev)

    def barrier(self):
        evs = [(self.sems[k], self.cnt[k]) for k in self.lists if self.cnt[k] > 0]
        evs += [(b.dsem, b.dcnt) for b in self.dma_bufs if b.dsem is not None and b.dcnt > 0]
        for q in self.lists:
            for ev in evs:
                if ev[0] is self.sems[q]:
                    continue
                self._wait(q, ev)

    def emit(self):
        nc = self.nc
        lists = self.lists
        self.lists = {k: [] for k in lists}
        self._emit(lists)

    def _emit(self, L):
        nc = self.nc
        with nc.Block() as block:
            @block.sync
            def _(e):
                for th in L["sync"]:
                    th(e)

            @block.scalar
            def _(e):
                for th in L["scalar"]:
                    th(e)

            @block.gpsimd
            def _(e):
                for th in L["gpsimd"]:
                    th(e)

            @block.vector
            def _(e):
                for th in L["vector"]:
                    th(e)

            @block.tensor
            def _(e):
                for th in L["tensor"]:
                    th(e)


class SB:
    def __init__(self, P, name, shape, dtype, psum=False, stack=None):
        st = stack if stack is not None else P.stack
        if psum:
            self.t = st.enter_context(P.nc.psum_tensor(name, shape, dtype))
        else:
            self.t = st.enter_context(P.nc.sbuf_tensor(name, shape, dtype))
        self.b = Buf(name)

    def __getitem__(self, k):
        return self.t[k]


class V:
    def __init__(self, ap, buf):
        self.t = ap
        self.b = buf

    def __getitem__(self, k):
        return self.t[k]


def build(cfg, stage=99):
    c = cfg
    D, KC, T, FC = c.D, c.KC, c.T, c.FC
    SEQ, CTX, SI, SH, HPG, DT = c.SEQ, c.CTX, c.SI, c.SH, c.HPG, c.DT
    SC = SI // 128
    NXC = c.XBC // 128
    MC = 16 + SC
    WK = max(KC, MC)
    NB = SEQ // 128
    NT = c.NT
    NTT = NT + 1
    NTOK = CTX + SEQ
    XG = 2
    nc = bass.Bass("TRN2", target_bir_lowering=False)

    def din(name, shape, dt=F32):
        return nc.dram_tensor(name, list(shape), dt, kind="ExternalInput").ap()

    def dscr(name, shape, dt=F32):
        return nc.dram_tensor(name, list(shape), dt).ap()

    xin_c = din("xin_c", [128, KC, CTX])
    xin_l = din("xin_l", [NT, 128, KC, T])
    cT = din("cT", [128, KC, 2])
    w_mod = din("w_mod", [D, 9 * D])
    b_modT = din("b_modT", [128, 9 * KC])
    npreT = din("npreT", [128, 3 * KC])
    npostT = din("npostT", [128, 3 * KC])
    wg = [din("wg%d" % i, [D, c.DFF]) for i in range(2)]
    wu = [din("wu%d" % i, [D, c.DFF]) for i in range(2)]
    wd = [din("wd%d" % i, [c.DFF, D]) for i in range(2)]
    w_in = din("w_in", [D, c.INC])
    w_out = din("w_out", [c.MW, D])
    cosT = din("cosT", [128, SEQ])
    sinT = din("sinT", [128, SEQ])
    consts = din("consts", [128, 7, 128])
    amask = din("amask", [128, 384])
    sinkb_d = din("sinkb_d", [128, 16])
    anormb_d = din("anormb_d", [128, c.AW])
    snormb_d = din("snormb_d", [128, SI])
    dtbb_d = din("dtbb_d", [128, DT])
    alogb_d = din("alogb_d", [128, DT])
    dskb_d = din("dskb_d", [128, SH])
    convw_d = din("convw_d", [128, NXC, 5])
    convb_d = din("convb_d", [128, NXC])
    out = nc.dram_tensor("out", [NT, 128, KC, T], F32, kind="ExternalOutput").ap()

    X1 = dscr("X1", [NT, 128, KC, T])
    QT = dscr("QT", [16, 128, SEQ], BF16)
    KT = dscr("KT", [4, 128, SEQ], BF16)
    KCT = dscr("KCT", [4, 128, CTX], BF16)
    VV = dscr("VV", [NTOK, 512], BF16)
    ZS = dscr("ZS", [SEQ, SI])
    XW = (CTX + 4) + (SEQ + 4)
    XBCA = dscr("XBCA", [NXC, 128, XW])
    DTA = dscr("DTA", [NTOK, DT])
    XS = dscr("XS", [NTOK, SI])
    BTS = dscr("BTS", [NTOK, 512], BF16)
    BFS = dscr("BFS", [4, 128, NTOK])
    CFS = dscr("CFS", [4, 128, NTOK])
    YA = dscr("YA", [SEQ, SI])
    MS = dscr("MS", [NB, 128, MC, 128], BF16)
    NFB = (FC + 1) // 2
    NBI = c.INC // 256
    WIS = dscr("WIS", [NBI, 128, KC, 256], BF16)
    WOS = dscr("WOS", [D // 256, 128, MC, 256], BF16)
    WGS = [dscr("WGS%d" % i, [NFB, 128, KC, 256], BF16) for i in range(2)]
    WUS = [dscr("WUS%d" % i, [NFB, 128, KC, 256], BF16) for i in range(2)]
    WDS = [dscr("WDS%d" % i, [NFB, 128, 2, D], BF16) for i in range(2)]

    stack = contextlib.ExitStack()
    with stack:
        stack.enter_context(nc.allow_low_precision("bf16 matmul operands, fp32 PSUM accumulation"))
        P = Prog(nc, stack)

        ones_f = SB(P, "ones_f", [128, 128], F32)
        epsb = SB(P, "epsb", [128, 1], F32)
        cst = SB(P, "cst", [128, 7, 128], F32)
        ident_b = SB(P, "ident_b", [128, 128], BF16)
        modT = SB(P, "modT", [128, 9 * KC, 2], F32)
        gs = SB(P, "gs", [128, 3 * KC, 2], F32)
        gp = SB(P, "gp", [128, 3 * KC, 2], F32)
        bm = SB(P, "bm", [128, 9 * KC], F32)
        npre = SB(P, "npre", [128, 3 * KC], F32)
        npost = SB(P, "npost", [128, 3 * KC], F32)
        cs_f = SB(P, "cs_f", [128, KC, 2], F32)
        cs_b = SB(P, "cs_b", [128, KC, 2], BF16)
        rstd = SB(P, "rstd", [128, T], F32)
        sq = [SB(P, "sq%d" % i, [128, T], F32) for i in range(1)]
        tmpf = [SB(P, "tmpf%d" % i, [128, T], F32) for i in range(2)]
        ps = [SB(P, "ps%d" % i, [128, 512], F32, psum=True) for i in range(8)]
        ident_f = cst[:, 0, :]
        Rm = cst[:, 1, :]
        tri = [cst[:, 2, :], cst[:, 3, :]]
        negm_ = [cst[:, 4, :], cst[:, 5, :]]

        P.op("vector", lambda e: e.memset(ones_f[:], 1.0), writes=[ones_f.b])
        P.op("vector", lambda e: e.memset(epsb[:], c.EPS), writes=[epsb.b])
        P.dma("sync", [(cst[:], consts[:, :, :])], writes=[cst.b], sem_owner=cst.b)
        P.op("vector", lambda e: e.tensor_copy(out=ident_b[:], in_=cst[:, 0, :]), reads=[cst.b], writes=[ident_b.b])

        FBC = 2

        def load_w(dst, src_view, rows, cols0, ncols):
            pairs = []
            step = max(1, min(8, (2 * 1024 * 1024) // (128 * ncols * 4)))
            for k0 in range(0, rows, step):
                k1 = min(rows, k0 + step)
                pairs.append((dst[:, k0:k1, 0:ncols], src_view[:, k0:k1, cols0:cols0 + ncols]))
            P.dma("gpsimd", pairs, writes=[dst.b], sem_owner=dst.b)

        def sumsq(chunk_ap, chunk_buf, idx, n, psb, tw):
            s_ = sq[0]
            P.op("scalar", lambda e: e.activation(out=s_[:, 0:tw], in_=chunk_ap, func=AF.Square),
                 reads=[chunk_buf], writes=[s_.b])
            P.op("tensor", lambda e: e.matmul(psb[:, 0:tw], lhsT=ones_f[:], rhs=s_[:, 0:tw],
                                              start=(idx == 0), stop=(idx == n - 1)),
                 reads=[s_.b, ones_f.b], writes=[psb.b])

        def finish_rstd(psb, tw, dim):
            P.op("scalar", lambda e: e.activation(out=rstd[:, 0:tw], in_=psb[:, 0:tw], func=AF.Ln, bias=epsb[:, 0:1], scale=1.0 / dim),
                 reads=[psb.b, epsb.b], writes=[rstd.b])
            P.op("scalar", lambda e: e.activation(out=rstd[:, 0:tw], in_=rstd[:, 0:tw], func=AF.Exp, scale=-0.5),
                 reads=[rstd.b], writes=[rstd.b])

        def stats_resident(src, tw):
            for kc in range(KC):
                sumsq(src[:, kc, 0:tw], src.b, kc, KC, ps[7], tw)
            finish_rstd(ps[7], tw, D)

        def mod_chunk(src_ap, src_buf, dst, kc, slot, r, tw):
            t_ = tmpf[kc % 2]
            P.op("vector", lambda e: e.scalar_tensor_tensor(
                out=t_[:, 0:tw], in0=src_ap, scalar=gs[:, slot * KC + kc, r:r + 1], in1=rstd[:, 0:tw],
                op0=ALU.mult, op1=ALU.mult), reads=[src_buf, gs.b, rstd.b], writes=[t_.b])
            P.op("scalar", lambda e: e.activation(
                out=dst[:, kc, 0:tw], in_=t_[:, 0:tw], func=AF.Identity,
                bias=modT[:, (3 * slot) * KC + kc, r:r + 1], scale=1.0),
                reads=[t_.b, modT.b], writes=[dst.b])

        def res_chunk(dst_ap, dst_buf, x_ap, x_buf, y_ap, y_buf, kc, slot, r, tw):
            t_ = tmpf[kc % 2]
            P.op("vector", lambda e: e.scalar_tensor_tensor(
                out=t_[:, 0:tw], in0=y_ap, scalar=gp[:, slot * KC + kc, r:r + 1], in1=rstd[:, 0:tw],
                op0=ALU.mult, op1=ALU.mult), reads=[y_buf, gp.b, rstd.b], writes=[t_.b])
            P.op("vector", lambda e: e.tensor_tensor(out=dst_ap, in0=x_ap, in1=t_[:, 0:tw], op=ALU.add),
                 reads=[t_.b, x_buf], writes=[dst_buf])

        def make_wbufs(stk, tag):
            WF = max(WK * 256, 2 * D)
            Wt = [SB(P, "%sW%d" % (tag, i), [128, WF], BF16, stack=stk) for i in range(4)]
            wblk = [V(Wt[i][:, 0:WK * 256].rearrange("p (kc n) -> p kc n", n=256), Wt[i].b) for i in range(4)]
            Hh = KC // 2
            ring = []
            for i in range(2):
                for h in range(2):
                    ring.append(V(Wt[i][:, h * Hh * 256:(h + 1) * Hh * 256].rearrange("p (kc n) -> p kc n", n=256), Wt[i].b))
            wdv_ = [V(Wt[2 + i][:, 0:2 * D].rearrange("p (a d) -> p a d", d=D), Wt[2 + i].b) for i in range(2)]
            return wblk, ring, wdv_

        def make_ffn(ht, yacc, ring, wdv_, sgts, at):
            Hh = KC // 2

            def ffn(i, tw):
                rctr = [0]

                def emit_D(pb_, dc):
                    a_ = at[pb_ % 2]
                    wd_ = wdv_[pb_ % 2]
                    nchp = min(FBC, FC - pb_ * FBC)
                    py = ps[4 + (dc % 3)]

                    def mmd(e):
                        ins = None
                        for j in range(nchp):
                            ins = e.matmul(py[:, 0:tw], lhsT=wd_[:, j, dc * 128:(dc + 1) * 128], rhs=a_[:, j, 0:tw],
                                           start=(j == 0), stop=(j == nchp - 1))
                        return ins
                    P.op("tensor", mmd, reads=[wd_.b, a_.b], writes=[py.b])
                    if pb_ == 0:
                        P.op("vector", lambda e: e.tensor_copy(out=yacc[:, dc, 0:tw], in_=py[:, 0:tw]),
                             reads=[py.b], writes=[yacc.b])
                    else:
                        P.op("vector", lambda e: e.tensor_tensor(
                            out=yacc[:, dc, 0:tw], in0=yacc[:, dc, 0:tw], in1=py[:, 0:tw], op=ALU.add),
                            reads=[py.b, yacc.b], writes=[yacc.b])

                for fb in range(NFB):
                    nch = min(FBC, FC - fb * FBC)
                    nco = nch * 128
                    hv = []
                    for srcd in (WGS[i], WUS[i]):
                        for h in range(2):
                            rb = ring[rctr[0] % 4]
                            rctr[0] += 1
                            P.dma("sync", [(rb[:, 0:Hh, 0:nco], srcd[fb, :, h * Hh:(h + 1) * Hh, 0:nco])],
                                  writes=[rb.b], sem_owner=rb.b)
                            hv.append(rb)
                    wd_ = wdv_[fb % 2]
                    P.dma("sync", [(wd_[:, 0:nch, :], WDS[i][fb, :, 0:nch, :])], writes=[wd_.b], sem_owner=wd_.b)
                    pgs = [ps[0], ps[1]]
                    pus = [ps[2], ps[3]]
                    slot = 0
                    for mi, (banks, halves) in enumerate(((pgs, hv[0:2]), (pus, hv[2:4]))):
                        for h in range(2):
                            rb = halves[h]
                            for j in range(nch):
                                pb = banks[j]
                                for k0 in range(0, Hh, 4):
                                    def mm(e, rb=rb, pb=pb, j=j, k0=k0, h=h):
                                        ins = None
                                        for kk in range(k0, min(Hh, k0 + 4)):
                                            kc = h * Hh + kk
                                            ins = e.matmul(pb[:, 0:tw], lhsT=rb[:, kk, j * 128:(j + 1) * 128], rhs=ht[:, kc, 0:tw],
                                                           start=(kc == 0), stop=(kc == KC - 1))
                                        return ins
                                    P.op("tensor", mm, reads=[rb.b, ht.b], writes=[pb.b])
                                    if fb >= 1 and slot < KC:
                                        emit_D(fb - 1, slot)
                                        slot += 1
                        if mi == 0:
                            for j in range(nch):
                                P.op("scalar", lambda e, j=j: e.activation(out=sgts[j][:, 0:tw], in_=pgs[j][:, 0:tw], func=AF.Silu),
                                     reads=[pgs[j].b], writes=[sgts[j].b])
                    if fb >= 1:
                        while slot < KC:
                            emit_D(fb - 1, slot)
                            slot += 1
                    a_ = at[fb % 2]
                    for j in range(nch):
                        P.op("vector", lambda e, j=j, a_=a_: e.tensor_tensor(
                            out=a_[:, j, 0:tw], in0=sgts[j][:, 0:tw], in1=pus[j][:, 0:tw], op=ALU.mult),
                            reads=[sgts[j].b, pus[j].b], writes=[a_.b])
                for dc in range(KC):
                    emit_D(NFB - 1, dc)
            return ffn

        def make_stream(xgs):
            ctr = [0]

            def stream(src, tw, body, dram_reads=()):
                for g0 in range(0, KC, XG):
                    n = min(XG, KC - g0)
                    xg = xgs[ctr[0] % 2]
                    ctr[0] += 1
                    P.dma("sync", [(xg[:, 0:n, 0:tw], src[:, g0:g0 + n, 0:tw])], reads=list(dram_reads), writes=[xg.b], sem_owner=xg.b)
                    for j in range(n):
                        body(g0 + j, xg, j)
            return stream

        phA = contextlib.ExitStack()
        with phA:
            yacc = SB(P, "yacc", [128, KC, T], F32, stack=phA)
            ht = SB(P, "ht", [128, KC, T], BF16, stack=phA)
            wblk, ring, wdv2 = make_wbufs(phA, "a")
            sgts = [sq[0], SB(P, "sg1", [128, T], F32, stack=phA)]
            at = [SB(P, "at%d" % i, [128, FBC, T], BF16, stack=phA) for i in range(2)]
            xgs = [SB(P, "xg%d" % i, [128, XG, T], F32, stack=phA) for i in range(2)]
            stgf = [V(xgs[i][:, 0, :], xgs[i].b) for i in range(2)]
            stgb = [V(at[i][:, 0, :], at[i].b) for i in range(2)]
            qf = [sq[0]]
            cos_t = V(xgs[0][:, 1, :], xgs[0].b)
            sin_t = V(xgs[1][:, 1, :], xgs[1].b)
            dtbb = SB(P, "dtbb", [128, DT], F32, stack=phA)
            zt = SB(P, "zt", [128, NXC, 2], F32, stack=phA)
            sp = [SB(P, "sp%d" % i, [128, DT], F32, stack=phA) for i in range(4)]
            ffn = make_ffn(ht, yacc, ring, wdv2, sgts, at)
            stream = make_stream(xgs)

            P.dma("sync", [(cs_f[:], cT[:, :, :])], writes=[cs_f.b], sem_owner=cs_f.b)
            P.dma("sync", [(bm[:], b_modT[:, :])], writes=[bm.b], sem_owner=bm.b)
            P.dma("sync", [(npre[:], npreT[:, :])], writes=[npre.b], sem_owner=npre.b)
            P.dma("sync", [(npost[:], npostT[:, :])], writes=[npost.b], sem_owner=npost.b)
            P.dma("sync", [(dtbb[:], dtbb_d[:, :])], writes=[dtbb.b], sem_owner=dtbb.b)
            P.op("scalar", lambda e: e.activation(out=cs_b[:], in_=cs_f[:], func=AF.Silu),
                 reads=[cs_f.b], writes=[cs_b.b])
            wmv = w_mod.rearrange("(kc p) n -> p kc n", p=128)
            nblk = 9 * KC // FBC
            for blk in range(nblk):
                wb = wblk[blk % 2]
                load_w(wb, wmv, KC, blk * 256, 256)
                pb = ps[blk % 2]
                for j in range(FBC):
                    fcidx = blk * FBC + j

                    def mm(e, wb=wb, pb=pb, j=j):
                        ins = None
                        for kc in range(KC):
                            ins = e.matmul(pb[:, j * 2:j * 2 + 2], lhsT=wb[:, kc, j * 128:(j + 1) * 128],
                                           rhs=cs_b[:, kc, :], start=(kc == 0), stop=(kc == KC - 1))
                        return ins
                    P.op("tensor", mm, reads=[wb.b, cs_b.b], writes=[pb.b])
                    P.op("vector", lambda e, pb=pb, j=j, fcidx=fcidx: e.tensor_scalar(
                        out=modT[:, fcidx, :], in0=pb[:, j * 2:j * 2 + 2], scalar1=bm[:, fcidx:fcidx + 1],
                        scalar2=None, op0=ALU.add), reads=[pb.b, bm.b], writes=[modT.b])
            for s in range(3):
                coef = 1.0 if s == 1 else 0.5
                for r in range(2):
                    P.op("vector", lambda e, s=s, r=r: e.scalar_tensor_tensor(
                        out=gs[:, s * KC:(s + 1) * KC, r], in0=modT[:, (3 * s + 1) * KC:(3 * s + 2) * KC, r], scalar=1.0,
                        in1=npre[:, s * KC:(s + 1) * KC], op0=ALU.add, op1=ALU.mult),
                        reads=[modT.b, npre.b], writes=[gs.b])
                    P.op("vector", lambda e, s=s, r=r, coef=coef: e.scalar_tensor_tensor(
                        out=gp[:, s * KC:(s + 1) * KC, r], in0=modT[:, (3 * s + 2) * KC:(3 * s + 3) * KC, r], scalar=coef,
                        in1=npost[:, s * KC:(s + 1) * KC], op0=ALU.mult, op1=ALU.mult),
                        reads=[modT.b, npost.b], writes=[gp.b])

            P.op("vector", lambda e: e.memset(zt[:], 0.0), writes=[zt.b])
            xv = XBCA.rearrange("c p t -> p c t")
            for c0 in (0, CTX + 2, CTX + 4, CTX + 4 + SEQ + 2):
                P.dma("sync", [(xv[:, x0:min(NXC, x0 + 4), c0:c0 + 2], zt[:, x0:min(NXC, x0 + 4), :]) for x0 in range(0, NXC, 4)],
                      reads=[zt.b], sem_owner=zt.b)

            wst = [Buf("wst%d" % k) for k in range(5)]
            pslots = [wblk[0], wblk[1], wblk[0], wblk[1]]
            pk = 0
            for i in range(2):
                wgv = wg[i].rearrange("(kc p) f -> p kc f", p=128)
                wuv = wu[i].rearrange("(kc p) f -> p kc f", p=128)
                wdv = wd[i].rearrange("(fc p) d -> p fc d", p=128)
                for fb in range(NFB):
                    nch = min(FBC, FC - fb * FBC)
                    nco = nch * 128
                    for (srcv, dstd) in ((wgv, WGS[i]), (wuv, WUS[i])):
                        sl = pslots[pk % 4]
                        load_w(sl, srcv, KC, fb * FBC * 128, nco)
                        P.dma("sync", [(dstd[fb, :, :, 0:nco], sl[:, 0:KC, 0:nco])], reads=[sl.b], sem_owner=wst[pk % 4])
                        pk += 1
                    wdb = wdv2[fb % 2]
                    P.dma("gpsimd", [(wdb[:, j, :], wdv[:, fb * FBC + j, :]) for j in range(nch)],
                          writes=[wdb.b], sem_owner=wdb.b)
                    P.dma("sync", [(WDS[i][fb, :, 0:nch, :], wdb[:, 0:nch, :])], reads=[wdb.b], sem_owner=wst[4])
            wiv0 = w_in.rearrange("(kc p) n -> p kc n", p=128)
            for bi in range(NBI):
                sl = pslots[pk % 4]
                load_w(sl, wiv0, KC, bi * 256, 256)
                P.dma("sync", [(WIS[bi], sl[:, 0:KC, :])], reads=[sl.b], sem_owner=wst[pk % 4])
                pk += 1
            wov0 = w_out.rearrange("(mc p) d -> p mc d", p=128)
            for bi in range(D // 256):
                sl = pslots[pk % 4]
                load_w(sl, wov0, MC, bi * 256, 256)
                P.dma("sync", [(WOS[bi], sl[:, 0:MC, :])], reads=[sl.b], sem_owner=wst[pk % 4])
                pk += 1
            P.barrier()

            wiv = w_in.rearrange("(kc p) n -> p kc n", p=128)
            wslots = [wblk[0], wblk[1], wblk[2], wblk[3]]
            wctr = [0]
            sctr = [0, 0]

            def next_w():
                w_ = wslots[wctr[0] % 4]
                wctr[0] += 1
                return w_

            def fm_group(col0, ncols, evac, tw):
                for b0 in range(0, ncols, 256):
                    nb_ = min(256, ncols - b0)
                    w_ = next_w()
                    if nb_ == 256 and (col0 + b0) % 256 == 0:
                        P.dma("gpsimd", [(w_[:, 0:KC, :], WIS[(col0 + b0) // 256])], writes=[w_.b], sem_owner=w_.b)
                    else:
                        load_w(w_, wiv, KC, col0 + b0, nb_)
                    for j in range(nb_ // 128):
                        pb = ps[(b0 // 128 + j) % 2]

                        def mm(e, w_=w_, pb=pb, j=j):
                            ins = None
                            for kc in range(KC):
                                ins = e.matmul(pb[:, 0:tw], lhsT=w_[:, kc, j * 128:(j + 1) * 128], rhs=ht[:, kc, 0:tw],
                                               start=(kc == 0), stop=(kc == KC - 1))
                            return ins
                        P.op("tensor", mm, reads=[w_.b, ht.b], writes=[pb.b])
                        evac(b0 // 128 + j, pb)

            def tm_group(col0, ncols, evac, tw):
                for b0 in range(0, ncols, 256):
                    nb_ = min(256, ncols - b0)
                    w_ = next_w()
                    if nb_ == 256 and (col0 + b0) % 256 == 0:
                        P.dma("gpsimd", [(w_[:, 0:KC, :], WIS[(col0 + b0) // 256])], writes=[w_.b], sem_owner=w_.b)
                    else:
                        load_w(w_, wiv, KC, col0 + b0, nb_)
                    for tg in range(tw // 128):
                        pb = ps[2 + (tg % 2)]

                        def mm(e, w_=w_, pb=pb, tg=tg, nb_=nb_):
                            ins = None
                            for kc in range(KC):
                                ins = e.matmul(pb[:, 0:nb_], lhsT=ht[:, kc, tg * 128:(tg + 1) * 128], rhs=w_[:, kc, 0:nb_],
                                               start=(kc == 0), stop=(kc == KC - 1))
                            return ins
                        P.op("tensor", mm, reads=[w_.b, ht.b], writes=[pb.b])
                        evac(tg, b0, nb_, pb)

            def nstg(kind):
                i = sctr[kind] % 2
                sctr[kind] += 1
                return (stgf if kind == 0 else stgb)[i]

            def inproj(ti, tw):
                is_ctx = (ti == 0)
                t0 = (ti - 1) * T
                if not is_ctx:
                    P.dma("sync", [(cos_t[:, 0:tw], cosT[:, t0:t0 + tw])], writes=[cos_t.b], sem_owner=cos_t.b)
                    P.dma("sync", [(sin_t[:, 0:tw], sinT[:, t0:t0 + tw])], writes=[sin_t.b], sem_owner=sin_t.b)

                def rope_evac(dst_fn):
                    def ev(ci, pb):
                        q_ = qf[0]
                        P.op("scalar", lambda e: e.activation(out=q_[:, 0:tw], in_=pb[:, 0:tw], func=AF.Copy),
                             reads=[pb.b], writes=[q_.b])
                        pr = ps[4 + (ci % 2)]
                        P.op("tensor", lambda e: e.matmul(pr[:, 0:tw], lhsT=Rm, rhs=q_[:, 0:tw], start=True, stop=True),
                             reads=[q_.b, cst.b], writes=[pr.b])
                        t1 = tmpf[0]
                        t2 = tmpf[1]
                        P.op("vector", lambda e: e.tensor_tensor(out=t1[:, 0:tw], in0=q_[:, 0:tw], in1=cos_t[:, 0:tw], op=ALU.mult),
                             reads=[q_.b, cos_t.b], writes=[t1.b])
                        P.op("vector", lambda e: e.tensor_tensor(out=t2[:, 0:tw], in0=pr[:, 0:tw], in1=sin_t[:, 0:tw], op=ALU.mult),
                             reads=[pr.b, sin_t.b], writes=[t2.b])
                        sb_ = nstg(1)
                        P.op("vector", lambda e: e.tensor_tensor(out=sb_[:, 0:tw], in0=t1[:, 0:tw], in1=t2[:, 0:tw], op=ALU.add),
                             reads=[t1.b, t2.b], writes=[sb_.b])
                        P.dma("sync", dst_fn(ci, sb_), reads=[sb_.b], sem_owner=sb_.b)
                    return ev

                if not is_ctx:
                    def qdst(ci, sb_):
                        return [(QT[ci, :, t0:t0 + tw], sb_[:, 0:tw])]
                    fm_group(0, c.AW, rope_evac(qdst), tw)

                    def zev(tg, b0, nb_, pb):
                        sf = nstg(0)
                        P.op("scalar", lambda e: e.activation(out=sf[:, 0:nb_], in_=pb[:, 0:nb_], func=AF.Silu),
                             reads=[pb.b], writes=[sf.b])
                        r0 = t0 + tg * 128
                        P.dma("sync", [(ZS[r0:r0 + 128, b0:b0 + nb_], sf[:, 0:nb_])], reads=[sf.b], sem_owner=sf.b)
                    tm_group(c.AW, SI, zev, tw)

                    def kdst(ci, sb_):
                        return [(KT[ci, :, t0:t0 + tw], sb_[:, 0:tw])]
                    fm_group(c.C0, c.KV, rope_evac(kdst), tw)
                else:
                    def kcev(ci, pb):
                        sb_ = nstg(1)
                        P.op("scalar", lambda e: e.activation(out=sb_[:, 0:tw], in_=pb[:, 0:tw], func=AF.Copy),
                             reads=[pb.b], writes=[sb_.b])
                        P.dma("sync", [(KCT[ci, :, :], sb_[:, 0:tw])], reads=[sb_.b], sem_owner=sb_.b)
                    fm_group(c.C0, c.KV, kcev, tw)
                vrow0 = 0 if is_ctx else CTX + t0

                def vev(tg, b0, nb_, pb):
                    sb_ = nstg(1)
                    P.op("vector", lambda e: e.tensor_copy(out=sb_[:, 0:nb_], in_=pb[:, 0:nb_]),
                         reads=[pb.b], writes=[sb_.b])
                    r0 = vrow0 + tg * 128
                    P.dma("sync", [(VV[r0:r0 + 128, b0:b0 + nb_], sb_[:, 0:nb_])], reads=[sb_.b], sem_owner=sb_.b)
                tm_group(c.C0 + c.KV, c.KV, vev, tw)
                xcol0 = 2 if is_ctx else (CTX + 4 + 2 + t0)

                def xev(ci, pb):
                    sf = nstg(0)
                    P.op("scalar", lambda e: e.activation(out=sf[:, 0:tw], in_=pb[:, 0:tw], func=AF.Copy),
                         reads=[pb.b], writes=[sf.b])
                    P.dma("sync", [(XBCA[ci, :, xcol0:xcol0 + tw], sf[:, 0:tw])], reads=[sf.b], sem_owner=sf.b)
                fm_group(c.C0 + 2 * c.KV, c.XBC, xev, tw)

                def dtev(tg, b0, nb_, pb):
                    x_, a_, m_, o_ = sp
                    P.op("vector", lambda e: e.tensor_tensor(out=x_[:], in0=pb[:, 0:DT], in1=dtbb[:], op=ALU.add),
                         reads=[pb.b, dtbb.b], writes=[x_.b])
                    P.op("scalar", lambda e: e.activation(out=a_[:], in_=x_[:], func=AF.Abs),
                         reads=[x_.b], writes=[a_.b])
                    P.op("scalar", lambda e: e.activation(out=a_[:], in_=a_[:], func=AF.Exp, scale=-1.0),
                         reads=[a_.b], writes=[a_.b])
                    P.op("scalar", lambda e: e.activation(out=a_[:], in_=a_[:], func=AF.Ln, bias=ones_f[:, 0:1], scale=1.0),
                         reads=[a_.b, ones_f.b], writes=[a_.b])
                    P.op("vector", lambda e: e.tensor_scalar_max(out=m_[:], in0=x_[:], scalar1=0.0),
                         reads=[x_.b], writes=[m_.b])
                    P.op("vector", lambda e: e.tensor_tensor(out=o_[:], in0=m_[:], in1=a_[:], op=ALU.add),
                         reads=[m_.b, a_.b], writes=[o_.b])
                    r0 = vrow0 + tg * 128
                    P.dma("sync", [(DTA[r0:r0 + 128, :], o_[:, :])], reads=[o_.b], sem_owner=o_.b)
                tm_group(c.C0 + 2 * c.KV + c.XBC, DT, dtev, tw)

            yacc_store = Buf("yacc_store")
            for ti in range(NTT):
                is_ctx = (ti == 0)
                r = 1 if is_ctx else 0
                tw = CTX if is_ctx else T
                src = xin_c if is_ctx else xin_l[ti - 1]
                stream(src, tw, lambda kc, xg, j: sumsq(xg[:, j, 0:tw], xg.b, kc, KC, ps[7], tw))
                finish_rstd(ps[7], tw, D)
                stream(src, tw, lambda kc, xg, j: mod_chunk(xg[:, j, 0:tw], xg.b, ht, kc, 0, r, tw))
                ffn(0, tw)
                stats_resident(yacc, tw)
                stream(src, tw, lambda kc, xg, j: res_chunk(yacc[:, kc, 0:tw], yacc.b, xg[:, j, 0:tw], xg.b,
                                                           yacc[:, kc, 0:tw], yacc.b, kc, 0, r, tw))
                if not is_ctx:
                    step = max(1, KC // 2)
                    P.dma("sync", [(X1[ti - 1, :, k0:k0 + step, :], yacc[:, k0:k0 + step, :]) for k0 in range(0, KC, step)],
                          reads=[yacc.b], sem_owner=yacc_store)
                stats_resident(yacc, tw)
                for kc in range(KC):
                    mod_chunk(yacc[:, kc, 0:tw], yacc.b, ht, kc, 1, r, tw)
                if stage >= 2:
                    inproj(ti, tw)
            P.barrier()
            P.emit()
        if stage <= 2:
            return nc

        NCH_ALL = NTOK // 128
        phB = contextlib.ExitStack()
        with phB:
            convw = SB(P, "convw", [128, NXC, 5], F32, stack=phB)
            convb = SB(P, "convb", [128, NXC], F32, stack=phB)
            win = [SB(P, "win%d" % i, [128, NXC, 132], F32, stack=phB) for i in range(2)]
            acc = [SB(P, "acc%d" % i, [128, 128], F32, stack=phB) for i in range(2)]
            cx = [SB(P, "cx%d" % i, [128, SC + 4, 128], F32, stack=phB) for i in range(2)]
            bfc = [SB(P, "bfc%d" % i, [128, 8, 128], F32, stack=phB) for i in range(2)]
            xs_s = [SB(P, "xs_s%d" % i, [128, SI], F32, stack=phB) for i in range(2)]
            bt_s = [SB(P, "bt_s%d" % i, [128, 512], BF16, stack=phB) for i in range(2)]
            P.dma("sync", [(convw[:], convw_d[:, :, :])], writes=[convw.b], sem_owner=convw.b)
            P.dma("sync", [(convb[:], convb_d[:, :])], writes=[convb.b], sem_owner=convb.b)
            xv = XBCA.rearrange("c p t -> p c t")
            def conv_chunk(ch):
                    s = ch % 2
                    if ch < CTX // 128:
                        base = ch * 128
                        tok0 = ch * 128
                    else:
                        base = CTX + 4 + (ch - CTX // 128) * 128
                        tok0 = ch * 128
                    w_ = win[s]
                    P.dma("sync", [(w_[:], xv[:, :, base:base + 132])], writes=[w_.b], sem_owner=w_.b)
                    cx_, bfc_ = cx[s], bfc[s]
                    for xc in range(NXC):
                        a_ = acc[xc % 2]
                        P.op("vector", lambda e, a_=a_, xc=xc, w_=w_: e.tensor_scalar(
                            out=a_[:], in0=w_[:, xc, 0:128], scalar1=convw[:, xc, 0:1], scalar2=None, op0=ALU.mult),
                            reads=[w_.b, convw.b], writes=[a_.b])
                        for k in range(1, 5):
                            P.op("vector", lambda e, a_=a_, xc=xc, w_=w_, k=k: e.scalar_tensor_tensor(
                                out=a_[:], in0=w_[:, xc, k:k + 128], scalar=convw[:, xc, k:k + 1], in1=a_[:],
                                op0=ALU.mult, op1=ALU.add), reads=[w_.b, convw.b, a_.b], writes=[a_.b])
                        if xc < SC + 4:
                            dstb, dst = cx_.b, cx_[:, xc, :]
                            P.op("scalar", lambda e, a_=a_, xc=xc, dst=dst: e.activation(
                                out=dst, in_=a_[:], func=AF.Silu, bias=convb[:, xc:xc + 1], scale=1.0),
                                reads=[a_.b, convb.b], writes=[dstb])
                            if xc >= SC:
                                P.op("vector", lambda e, xc=xc, cx_=cx_, bfc_=bfc_: e.tensor_copy(
                                    out=bfc_[:, xc - SC, :], in_=cx_[:, xc, :]), reads=[cx_.b], writes=[bfc_.b])
                        else:
                            P.op("scalar", lambda e, a_=a_, xc=xc, bfc_=bfc_: e.activation(
                                out=bfc_[:, xc - SC, :], in_=a_[:], func=AF.Silu, bias=convb[:, xc:xc + 1], scale=1.0),
                                reads=[a_.b, convb.b], writes=[bfc_.b])
                    xs_ = xs_s[s]
                    for g0 in range(0, SC, 4):
                        n_ = min(4, SC - g0)
                        pt = ps[(g0 // 4) % 2]

                        def tr(e, g0=g0, n_=n_, pt=pt, cx_=cx_):
                            ins = None
                            for j in range(n_):
                                ins = e.transpose(pt[:, j * 128:(j + 1) * 128], cx_[:, g0 + j, :], ident_f)
                            return ins
                        P.op("tensor", tr, reads=[cx_.b, cst.b], writes=[pt.b])
                        P.op("vector", lambda e, g0=g0, n_=n_, pt=pt, xs_=xs_: e.tensor_copy(
                            out=xs_[:, g0 * 128:(g0 + n_) * 128], in_=pt[:, 0:n_ * 128]), reads=[pt.b], writes=[xs_.b])
                    pt = ps[2]

                    def trb(e, pt=pt, cx_=cx_):
                        ins = None
                        for j in range(4):
                            ins = e.transpose(pt[:, j * 128:(j + 1) * 128], cx_[:, SC + j, :], ident_f)
                        return ins
                    P.op("tensor", trb, reads=[cx_.b, cst.b], writes=[pt.b])
                    bt_ = bt_s[s]
                    P.op("vector", lambda e, pt=pt, bt_=bt_: e.tensor_copy(out=bt_[:], in_=pt[:, 0:512]),
                         reads=[pt.b], writes=[bt_.b])
                    P.dma("sync", [(XS[tok0:tok0 + 128, :], xs_[:])], reads=[xs_.b], sem_owner=xs_.b)
                    P.dma("sync", [(BTS[tok0:tok0 + 128, :], bt_[:])], reads=[bt_.b], sem_owner=bt_.b)
                    bfv = BFS.rearrange("g p t -> p g t")
                    cfv = CFS.rearrange("g p t -> p g t")
                    P.dma("sync", [(bfv[:, :, tok0:tok0 + 128], bfc_[:, 0:4, :]), (cfv[:, :, tok0:tok0 + 128], bfc_[:, 4:8, :])],
                          reads=[bfc_.b], sem_owner=bfc_.b)

            for ch in range(NCH_ALL):
                conv_chunk(ch)
            P.barrier()
            P.emit()

        if stage == 3:
            return nc
        phC = contextlib.ExitStack()
        with phC:
            HW = HPG * 64
            hst = [SB(P, "hst%d" % i, [128, 4, HW], F32, stack=phC) for i in range(2)]
            hsb = [SB(P, "hsb%d" % i, [128, 4, HW], BF16, stack=phC) for i in range(2)]
            abc = SB(P, "abc", [128, DT], F32, stack=phC)
            dskb = SB(P, "dskb", [128, SH], F32, stack=phC)
            snormb = SB(P, "snormb", [128, SI], F32, stack=phC)
            xs_l = [SB(P, "xs_l%d" % i, [128, SI], F32, stack=phC) for i in range(2)]
            bt_l = [SB(P, "bt_l%d" % i, [128, 512], BF16, stack=phC) for i in range(2)]
            bc_l = [SB(P, "bc_l%d" % i, [128, 8, 128], BF16, stack=phC) for i in range(2)]
            dt_l = [SB(P, "dt_l%d" % i, [128, DT], F32, stack=phC) for i in range(2)]
            ya_l = [SB(P, "ya_l%d" % i, [128, SI], F32, stack=phC) for i in range(2)]
            zs_l = [SB(P, "zs_l%d" % i, [128, SI], F32, stack=phC) for i in range(2)]
            dA = SB(P, "dA", [128, SH], F32, stack=phC)
            acs = SB(P, "acs", [128, SH], F32, stack=phC)
            dte = SB(P, "dte", [128, SH], F32, stack=phC)
            edec = SB(P, "edec", [128, SH], F32, stack=phC)
            xdt = SB(P, "xdt", [128, SI], F32, stack=phC)
            xdt_b = SB(P, "xdt_b", [128, SI], BF16, stack=phC)
            xdtd_b = SB(P, "xdtd_b", [128, SI], BF16, stack=phC)
            cb = SB(P, "cb", [128, 128], F32, stack=phC)
            rhsA = SB(P, "rhsA", [128, HPG, 128], F32, stack=phC)
            larg = SB(P, "larg", [128, HPG, 128], F32, stack=phC)
            mt = SB(P, "mt", [128, HPG, 128], BF16, stack=phC)
            ec = SB(P, "ec", [128, HPG, 128], F32, stack=phC)
            cs_ = SB(P, "cs_", [128, HPG, 128], BF16, stack=phC)
            yrow = SB(P, "yrow", [128, SI], F32, stack=phC)
            ysq = SB(P, "ysq", [128, SI], F32, stack=phC)
            ssq = SB(P, "ssq", [128, 1], F32, stack=phC)
            rsd = SB(P, "rsd", [128, 1], F32, stack=phC)
            sn_b = SB(P, "sn_b", [128, SI], BF16, stack=phC)
            st_s = [SB(P, "st_s%d" % i, [128, SC, 128], BF16, stack=phC) for i in range(2)]
            tmpd = SB(P, "tmpd", [128, 4, HW], F32, stack=phC)
            eacs = SB(P, "eacs", [128, SH], F32, stack=phC)
            tmpy = SB(P, "tmpy", [128, HW], F32, stack=phC)
            atot = SB(P, "atot", [128, SH], F32, stack=phC)

            P.dma("sync", [(abc[:], alogb_d[:, :])], writes=[abc.b], sem_owner=abc.b)
            P.dma("sync", [(dskb[:], dskb_d[:, :])], writes=[dskb.b], sem_owner=dskb.b)
            P.dma("sync", [(snormb[:], snormb_d[:, :])], writes=[snormb.b], sem_owner=snormb.b)
            P.op("scalar", lambda e: e.activation(out=abc[:], in_=abc[:], func=AF.Exp), reads=[abc.b], writes=[abc.b])
            P.op("vector", lambda e: e.tensor_single_scalar(out=abc[:], in_=abc[:], scalar=-1.0, op=ALU.mult),
                 reads=[abc.b], writes=[abc.b])
            for d_ in range(2):
                P.op("vector", lambda e, d_=d_: e.memset(hst[d_][:], 0.0), writes=[hst[d_].b])
                P.op("vector", lambda e, d_=d_: e.memset(hsb[d_][:], 0.0), writes=[hsb[d_].b])

            bfv = BFS.rearrange("g p t -> p g t")
            cfv = CFS.rearrange("g p t -> p g t")
            nA = (HPG * 128 + 511) // 512
            lctr = [0]

            def ssd_chunk(tok0, d_, need_y, lat_idx):
                s = lctr[0] % 2
                lctr[0] += 1
                xs_, bt_, bc_, dt_ = xs_l[s], bt_l[s], bc_l[s], dt_l[s]
                P.dma("sync", [(xs_[:], XS[tok0:tok0 + 128, :])], writes=[xs_.b], sem_owner=xs_.b)
                P.dma("sync", [(bt_[:], BTS[tok0:tok0 + 128, :])], writes=[bt_.b], sem_owner=bt_.b)
                P.dma("gpsimd", [(bc_[:, 0:4, :], bfv[:, :, tok0:tok0 + 128]), (bc_[:, 4:8, :], cfv[:, :, tok0:tok0 + 128])],
                      writes=[bc_.b], sem_owner=bc_.b)
                P.dma("sync", [(dt_[:], DTA[tok0:tok0 + 128, :])], writes=[dt_.b], sem_owner=dt_.b)
                dts = dt_[:, d_ * SH:(d_ + 1) * SH]
                P.op("vector", lambda e: e.tensor_tensor(out=dA[:], in0=dts, in1=abc[:, d_ * SH:(d_ + 1) * SH], op=ALU.mult),
                     reads=[dt_.b, abc.b], writes=[dA.b])
                pA, pT = ps[0], ps[1]
                P.op("tensor", lambda e: e.matmul(pA[:, 0:SH], lhsT=tri[d_], rhs=dA[:], start=True, stop=True),
                     reads=[dA.b, cst.b], writes=[pA.b])
                P.op("tensor", lambda e: e.matmul(pT[:, 0:SH], lhsT=ones_f[:], rhs=dA[:], start=True, stop=True),
                     reads=[dA.b, ones_f.b], writes=[pT.b])
                P.op("vector", lambda e: e.tensor_copy(out=acs[:], in_=pA[:, 0:SH]), reads=[pA.b], writes=[acs.b])
                P.op("vector", lambda e: e.tensor_copy(out=atot[:], in_=pT[:, 0:SH]), reads=[pT.b], writes=[atot.b])
                P.op("vector", lambda e: e.tensor_tensor(out=dte[:], in0=atot[:], in1=acs[:], op=ALU.subtract),
                     reads=[atot.b, acs.b], writes=[dte.b])
                P.op("scalar", lambda e: e.activation(out=dte[:], in_=dte[:], func=AF.Exp), reads=[dte.b], writes=[dte.b])
                P.op("scalar", lambda e: e.activation(out=edec[:], in_=atot[:], func=AF.Exp), reads=[atot.b], writes=[edec.b])
                x3 = xs_[:].rearrange("p (h d) -> p h d", d=64)
                P.op("vector", lambda e: e.tensor_tensor(
                    out=xdt[:].rearrange("p (h d) -> p h d", d=64), in0=x3,
                    in1=dts.unsqueeze(2).to_broadcast([128, SH, 64]), op=ALU.mult),
                    reads=[xs_.b, dt_.b], writes=[xdt.b])
                P.op("scalar", lambda e: e.activation(out=xdt_b[:], in_=xdt[:], func=AF.Copy), reads=[xdt.b], writes=[xdt_b.b])
                P.op("vector", lambda e: e.tensor_tensor(
                    out=xdtd_b[:].rearrange("p (h d) -> p h d", d=64), in0=xdt[:].rearrange("p (h d) -> p h d", d=64),
                    in1=dte[:].unsqueeze(2).to_broadcast([128, SH, 64]), op=ALU.mult),
                    reads=[xdt.b, dte.b], writes=[xdtd_b.b])
                if need_y and (SSDM & 1):
                    def yop(*a, **k):
                        yc[0] += 1
                        if yc[0] <= YSTEPS:
                            P.op(*a, **k)
                    yc = [0]
                    yop("scalar", lambda e: e.activation(out=eacs[:], in_=acs[:], func=AF.Exp), reads=[acs.b], writes=[eacs.b])
                    for g in range(4):
                        yc = [1]
                        hs = slice(g * HPG, (g + 1) * HPG)
                        pC = ps[2]
                        yop("tensor", lambda e, g=g: e.matmul(pC[:, 0:128], lhsT=bc_[:, g, :], rhs=bc_[:, 4 + g, :],
                                                              start=True, stop=True), reads=[bc_.b], writes=[pC.b])
                        yop("scalar", lambda e: e.activation(out=cb[:], in_=pC[:, 0:128], func=AF.Copy),
                            reads=[pC.b], writes=[cb.b])
                        yop("vector", lambda e, hs=hs: e.tensor_tensor(
                            out=rhsA[:], in0=tri[d_].unsqueeze(1).to_broadcast([128, HPG, 128]),
                            in1=dA[:, hs].unsqueeze(2).to_broadcast([128, HPG, 128]), op=ALU.mult),
                            reads=[dA.b, cst.b], writes=[rhsA.b])
                        pAs = [ps[3], ps[4]][:nA]
                        rflat = rhsA[:].rearrange("p h i -> p (h i)")
                        for bi, pb in enumerate(pAs):
                            w0 = bi * 512
                            w1 = min(HPG * 128, w0 + 512)
                            yop("tensor", lambda e, pb=pb, w0=w0, w1=w1: e.matmul(
                                pb[:, 0:w1 - w0], lhsT=ones_f[:], rhs=rflat[:, w0:w1], start=True, stop=True),
                                reads=[rhsA.b, ones_f.b], writes=[pb.b])
                        for bi, pb in enumerate(pAs):
                            h0 = bi * 4
                            nh_ = min(HPG - h0, 4)
                            for hh in range(nh_):
                                col = g * HPG + h0 + hh
                                yop("vector", lambda e, pb=pb, hh=hh, h0=h0, col=col: e.tensor_scalar(
                                    out=larg[:, h0 + hh, :], in0=pb[:, hh * 128:(hh + 1) * 128], scalar1=acs[:, col:col + 1],
                                    scalar2=None, op0=ALU.subtract), reads=[pb.b, acs.b], writes=[larg.b])
                        yop("vector", lambda e: e.tensor_tensor(
                            out=larg[:], in0=larg[:], in1=negm_[d_].unsqueeze(1).to_broadcast([128, HPG, 128]), op=ALU.add),
                            reads=[larg.b, cst.b], writes=[larg.b])
                        yop("scalar", lambda e: e.activation(out=larg[:], in_=larg[:], func=AF.Exp),
                            reads=[larg.b], writes=[larg.b])
                        yop("vector", lambda e: e.tensor_tensor(
                            out=mt[:], in0=larg[:], in1=cb[:].unsqueeze(1).to_broadcast([128, HPG, 128]), op=ALU.mult),
                            reads=[larg.b, cb.b], writes=[mt.b])
                        pYd, pYo = ps[5], ps[6]

                        def ymm(e, g=g):
                            ins = None
                            for h in range(HPG):
                                hh = g * HPG + h
                                ins = e.matmul(pYd[:, h * 64:(h + 1) * 64], lhsT=mt[:, h, :], rhs=xdt_b[:, hh * 64:(hh + 1) * 64],
                                               start=True, stop=True)
                            return ins
                        yop("tensor", ymm, reads=[mt.b, xdt_b.b], writes=[pYd.b])
                        yop("tensor", lambda e, g=g: e.matmul(pYo[:, 0:HW], lhsT=bc_[:, 4 + g, :], rhs=hsb[d_][:, g, :],
                                                              start=True, stop=True), reads=[bc_.b, hsb[d_].b], writes=[pYo.b])
                        yop("vector", lambda e, g=g: e.tensor_tensor(
                            out=tmpy[:].rearrange("p (h d) -> p h d", d=64),
                            in0=pYo[:, 0:HW].rearrange("p (h d) -> p h d", d=64),
                            in1=eacs[:, g * HPG:(g + 1) * HPG].unsqueeze(2).to_broadcast([128, HPG, 64]), op=ALU.mult),
                            reads=[pYo.b, eacs.b], writes=[tmpy.b])
                        ysl = yrow[:, g * HW:(g + 1) * HW]
                        if d_ == 0:
                            yop("vector", lambda e, g=g, ysl=ysl: e.tensor_tensor(
                                out=ysl.rearrange("p (h d) -> p h d", d=64),
                                in0=xs_[:, g * HW:(g + 1) * HW].rearrange("p (h d) -> p h d", d=64),
                                in1=dskb[:, g * HPG:(g + 1) * HPG].unsqueeze(2).to_broadcast([128, HPG, 64]), op=ALU.mult),
                                reads=[xs_.b, dskb.b], writes=[yrow.b])
                            yop("vector", lambda e, ysl=ysl: e.tensor_tensor(out=ysl, in0=ysl, in1=tmpy[:], op=ALU.add),
                                reads=[tmpy.b, yrow.b], writes=[yrow.b])
                        else:
                            ya_ = ya_l[s]
                            yop("vector", lambda e, g=g, ysl=ysl, ya_=ya_: e.tensor_tensor(
                                out=ysl, in0=ya_[:, g * HW:(g + 1) * HW], in1=tmpy[:], op=ALU.add),
                                reads=[tmpy.b, ya_.b], writes=[yrow.b])
                        yop("vector", lambda e, ysl=ysl: e.tensor_tensor(out=ysl, in0=ysl, in1=pYd[:, 0:HW], op=ALU.add),
                            reads=[pYd.b, yrow.b], writes=[yrow.b])
                for g in range(4):
                    pS = ps[7]
                    P.op("tensor", lambda e, g=g: e.matmul(pS[:, 0:HW], lhsT=bt_[:, g * 128:(g + 1) * 128],
                                                           rhs=xdtd_b[:, g * HW:(g + 1) * HW], start=True, stop=True),
                         reads=[bt_.b, xdtd_b.b], writes=[pS.b])
                    P.op("vector", lambda e, g=g: e.tensor_tensor(
                        out=tmpd[:, g, :].rearrange("p (h d) -> p h d", d=64),
                        in0=hst[d_][:, g, :].rearrange("p (h d) -> p h d", d=64),
                        in1=edec[:, g * HPG:(g + 1) * HPG].unsqueeze(2).to_broadcast([128, HPG, 64]), op=ALU.mult),
                        reads=[hst[d_].b, edec.b, hsb[d_].b], writes=[tmpd.b])
                    P.op("vector", lambda e, g=g: e.tensor_tensor(out=hst[d_][:, g, :], in0=tmpd[:, g, :], in1=pS[:, 0:HW], op=ALU.add),
                         reads=[tmpd.b, pS.b], writes=[hst[d_].b])
                P.op("scalar", lambda e: e.activation(out=hsb[d_][:], in_=hst[d_][:], func=AF.Copy),
                     reads=[hst[d_].b], writes=[hsb[d_].b])
                return s

            ncc = CTX // 128
            for ch in range(ncc):
                ssd_chunk(ch * 128, 0, False, None)
            for ch in reversed(range(ncc)):
                ssd_chunk(ch * 128, 1, False, None)
            for lb in range(NB):
                s = ssd_chunk(CTX + lb * 128, 0, True, lb)
                P.dma("sync", [(YA[lb * 128:(lb + 1) * 128, :], yrow[:])], reads=[yrow.b], sem_owner=yrow.b)
            P.barrier()
            for lb in reversed(range(NB)):
                s = lctr[0] % 2
                ya_, zs_ = ya_l[s], zs_l[s]
                P.dma("sync", [(ya_[:], YA[lb * 128:(lb + 1) * 128, :])], writes=[ya_.b], sem_owner=ya_.b)
                P.dma("sync", [(zs_[:], ZS[lb * 128:(lb + 1) * 128, :])], writes=[zs_.b], sem_owner=zs_.b)
                ssd_chunk(CTX + lb * 128, 1, True, lb)
                if not (SSDM & 2):
                    continue
                P.op("vector", lambda e, zs_=zs_: e.tensor_tensor(out=yrow[:], in0=yrow[:], in1=zs_[:], op=ALU.mult),
                     reads=[yrow.b, zs_.b], writes=[yrow.b])
                P.op("vector", lambda e: e.memset(ssq[:], 0.0), writes=[ssq.b])
                P.op("scalar", lambda e: e.activation(out=ysq[:], in_=yrow[:], func=AF.Square, accum_out=ssq[:]),
                     reads=[yrow.b, ssq.b], writes=[ysq.b, ssq.b])
                P.op("scalar", lambda e: e.activation(out=rsd[:], in_=ssq[:], func=AF.Ln, bias=epsb[:, 0:1], scale=1.0 / SI),
                     reads=[ssq.b, epsb.b], writes=[rsd.b])
                P.op("scalar", lambda e: e.activation(out=rsd[:], in_=rsd[:], func=AF.Exp, scale=-0.5),
                     reads=[rsd.b], writes=[rsd.b])
                P.op("vector", lambda e: e.scalar_tensor_tensor(out=sn_b[:], in0=yrow[:], scalar=rsd[:, 0:1], in1=snormb[:],
                                                                op0=ALU.mult, op1=ALU.mult),
                     reads=[yrow.b, rsd.b, snormb.b], writes=[sn_b.b])
                st_ = st_s[lb % 2]
                for g0 in range(0, SC, 8):
                    n_ = min(8, SC - g0)
                    pt = ps[5 + ((g0 // 8) % 2)]
                    ptb = pt[:].bitcast(BF16)

                    def tr(e, g0=g0, n_=n_, ptb=ptb):
                        ins = None
                        for j in range(n_):
                            ins = e.transpose(ptb[:, j * 128:(j + 1) * 128], sn_b[:, (g0 + j) * 128:(g0 + j + 1) * 128], ident_b[:])
                        return ins
                    P.op("tensor", tr, reads=[sn_b.b, ident_b.b], writes=[pt.b])
                    P.op("vector", lambda e, g0=g0, n_=n_, ptb=ptb, st_=st_: e.tensor_copy(
                        out=st_[:, g0:g0 + n_, :].rearrange("p c t -> p (c t)"), in_=ptb[:, 0:n_ * 128]),
                        reads=[pt.b], writes=[st_.b])
                P.dma("sync", [(MS[lb, :, 16:16 + SC, :], st_[:])], reads=[st_.b], sem_owner=st_.b)
            P.barrier()
            P.emit()

        if stage == 4:
            return nc
        phD = contextlib.ExitStack()
        with phD:
            NKMAX = 384 + CTX
            q_sb = [SB(P, "q_sb%d" % i, [128, 16, 256], BF16, stack=phD) for i in range(2)]
            k_sb = [SB(P, "k_sb%d" % i, [128, 4, 384], BF16, stack=phD) for i in range(2)]
            v_sb = [SB(P, "v_sb%d" % i, [128, 3, 512], BF16, stack=phD) for i in range(2)]
            kc_sb = SB(P, "kc_sb", [128, 4, CTX], BF16, stack=phD)
            vc_sb = SB(P, "vc_sb", [128, CTX // 128, 512], BF16, stack=phD)
            maskt = SB(P, "maskt", [128, 384], F32, stack=phD)
            sinkb = SB(P, "sinkb", [128, 16], F32, stack=phD)
            anormb = SB(P, "anormb", [128, c.AW], F32, stack=phD)
            s_sb = [SB(P, "s_sb%d" % i, [128, NKMAX], F32, stack=phD) for i in range(2)]
            p_sb = [SB(P, "p_sb%d" % i, [128, NKMAX], BF16, stack=phD) for i in range(2)]
            pt_sb = [SB(P, "pt_sb%d" % i, [128, NKMAX], BF16, stack=phD) for i in range(2)]
            sm = [SB(P, "sm%d" % i, [128, 8], F32, stack=phD) for i in range(2)]
            arow = SB(P, "arow", [128, c.AW], F32, stack=phD)
            asq = SB(P, "asq", [128, c.AW], F32, stack=phD)
            an_b = SB(P, "an_b", [128, c.AW], BF16, stack=phD)
            at_s = [SB(P, "at_s%d" % i, [128, 16, 128], BF16, stack=phD) for i in range(2)]
            assq = SB(P, "assq", [128, 2], F32, stack=phD)
            P.dma("sync", [(kc_sb[:], KCT.rearrange("g p t -> p g t"))], writes=[kc_sb.b], sem_owner=kc_sb.b)
            P.dma("sync", [(vc_sb[:], VV[0:CTX, :].rearrange("(b p) c -> p b c", p=128))], writes=[vc_sb.b], sem_owner=vc_sb.b)
            P.dma("sync", [(maskt[:], amask[:, :])], writes=[maskt.b], sem_owner=maskt.b)
            P.dma("sync", [(sinkb[:], sinkb_d[:, :])], writes=[sinkb.b], sem_owner=sinkb.b)
            P.dma("sync", [(anormb[:], anormb_d[:, :])], writes=[anormb.b], sem_owner=anormb.b)
            ktv = KT.rearrange("g p t -> p g t")
            vlv = VV[CTX:CTX + SEQ, :].rearrange("(b p) c -> p b c", p=128)
            scale = float(c.DH) ** -0.5
            def attn_block(qb):
                    s = qb % 2
                    kb0, kb1 = max(qb - 1, 0), min(qb + 1, NB - 1)
                    nkb = kb1 - kb0 + 1
                    nband = nkb * 128
                    m0 = 0 if qb > 0 else 128
                    ntot = nband + CTX
                    nblk = nkb + CTX // 128
                    q_, k_, v_ = q_sb[(qb // 2) % 2], k_sb[s], v_sb[s]
                    qo = (qb % 2) * 128
                    if qb % 2 == 0:
                        P.dma("sync", [(q_[:], QT.rearrange("h p t -> p h t")[:, :, qb * 128:(qb + 2) * 128])], writes=[q_.b], sem_owner=q_.b)
                    P.dma("sync", [(k_[:, :, 0:nband], ktv[:, :, kb0 * 128:(kb1 + 1) * 128])], writes=[k_.b], sem_owner=k_.b)
                    P.dma("sync", [(v_[:, 0:nkb, :], vlv[:, kb0:kb1 + 1, :])], writes=[v_.b], sem_owner=v_.b)
                    for h in range(16):
                        g = h // 4
                        hs_ = h % 2
                        pS, pC = ps[0 + hs_], ps[2 + hs_]
                        ssb, psb_, ptsb, sm_ = s_sb[hs_], p_sb[hs_], pt_sb[hs_], sm[hs_]
                        P.op("tensor", lambda e, h=h, g=g, pS=pS: e.matmul(pS[:, 0:nband], lhsT=q_[:, h, qo:qo + 128], rhs=k_[:, g, 0:nband],
                                                                           start=True, stop=True), reads=[q_.b, k_.b], writes=[pS.b])
                        P.op("tensor", lambda e, h=h, g=g, pC=pC: e.matmul(pC[:, 0:CTX], lhsT=q_[:, h, qo:qo + 128], rhs=kc_sb[:, g, :],
                                                                           start=True, stop=True), reads=[q_.b, kc_sb.b], writes=[pC.b])
                        P.op("vector", lambda e, pS=pS, ssb=ssb: e.scalar_tensor_tensor(
                            out=ssb[:, 0:nband], in0=pS[:, 0:nband], scalar=scale, in1=maskt[:, m0:m0 + nband],
                            op0=ALU.mult, op1=ALU.add), reads=[pS.b, maskt.b], writes=[ssb.b])
                        P.op("scalar", lambda e, pC=pC, ssb=ssb: e.activation(out=ssb[:, nband:ntot], in_=pC[:, 0:CTX], func=AF.Copy, scale=scale),
                             reads=[pC.b], writes=[ssb.b])
                        P.op("vector", lambda e, ssb=ssb, sm_=sm_: e.reduce_max(out=sm_[:, 0:1], in_=ssb[:, 0:ntot], axis=AX.X),
                             reads=[ssb.b], writes=[sm_.b])
                        P.op("vector", lambda e, sm_=sm_, h=h: e.tensor_tensor(out=sm_[:, 0:1], in0=sm_[:, 0:1], in1=sinkb[:, h:h + 1], op=ALU.max),
                             reads=[sm_.b, sinkb.b], writes=[sm_.b])
                        P.op("vector", lambda e, sm_=sm_: e.tensor_single_scalar(out=sm_[:, 1:2], in_=sm_[:, 0:1], scalar=-1.0, op=ALU.mult),
                             reads=[sm_.b], writes=[sm_.b])
                        P.op("vector", lambda e, sm_=sm_: e.memset(sm_[:, 2:3], 0.0), reads=[sm_.b], writes=[sm_.b])
                        P.op("scalar", lambda e, ssb=ssb, psb_=psb_, sm_=sm_: e.activation(
                            out=psb_[:, 0:ntot], in_=ssb[:, 0:ntot], func=AF.Exp, bias=sm_[:, 1:2], scale=1.0, accum_out=sm_[:, 2:3]),
                            reads=[ssb.b, sm_.b], writes=[psb_.b, sm_.b])
                        P.op("scalar", lambda e, sm_=sm_, h=h: e.activation(out=sm_[:, 3:4], in_=sinkb[:, h:h + 1], func=AF.Exp, bias=sm_[:, 1:2], scale=1.0),
                             reads=[sm_.b, sinkb.b], writes=[sm_.b])
                        P.op("vector", lambda e, sm_=sm_: e.tensor_tensor(out=sm_[:, 4:5], in0=sm_[:, 2:3], in1=sm_[:, 3:4], op=ALU.add),
                             reads=[sm_.b], writes=[sm_.b])
                        P.op("vector", lambda e, sm_=sm_: e.reciprocal(out=sm_[:, 5:6], in_=sm_[:, 4:5]), reads=[sm_.b], writes=[sm_.b])
                        pT = ps[4 + hs_]
                        pTb = pT[:].bitcast(BF16)

                        def trp(e, pTb=pTb, psb_=psb_):
                            ins = None
                            for j in range(nblk):
                                ins = e.transpose(pTb[:, j * 128:(j + 1) * 128], psb_[:, j * 128:(j + 1) * 128], ident_b[:])
                            return ins
                        P.op("tensor", trp, reads=[psb_.b, ident_b.b], writes=[pT.b])
                        P.op("scalar", lambda e, pTb=pTb, ptsb=ptsb: e.activation(out=ptsb[:, 0:ntot], in_=pTb[:, 0:ntot], func=AF.Copy),
                             reads=[pT.b], writes=[ptsb.b])
                        pO = ps[6 + hs_]

                        def pv(e, g=g, pO=pO, ptsb=ptsb):
                            ins = None
                            for j in range(nblk):
                                if j < nkb:
                                    vv_ = v_[:, j, g * 128:(g + 1) * 128]
                                else:
                                    vv_ = vc_sb[:, j - nkb, g * 128:(g + 1) * 128]
                                ins = e.matmul(pO[:, 0:128], lhsT=ptsb[:, j * 128:(j + 1) * 128], rhs=vv_,
                                               start=(j == 0), stop=(j == nblk - 1))
                            return ins
                        P.op("tensor", pv, reads=[ptsb.b, v_.b, vc_sb.b], writes=[pO.b])
                        P.op("vector", lambda e, pO=pO, sm_=sm_, h=h: e.tensor_scalar(
                            out=arow[:, h * 128:(h + 1) * 128], in0=pO[:, 0:128], scalar1=sm_[:, 5:6], scalar2=None, op0=ALU.mult),
                            reads=[pO.b, sm_.b], writes=[arow.b])
                    P.op("vector", lambda e: e.memset(assq[:, 0:1], 0.0), writes=[assq.b])
                    P.op("scalar", lambda e: e.activation(out=asq[:], in_=arow[:], func=AF.Square, accum_out=assq[:, 0:1]),
                         reads=[arow.b, assq.b], writes=[asq.b, assq.b])
                    P.op("scalar", lambda e: e.activation(out=assq[:, 1:2], in_=assq[:, 0:1], func=AF.Ln, bias=epsb[:, 0:1], scale=1.0 / c.AW),
                         reads=[assq.b, epsb.b], writes=[assq.b])
                    P.op("scalar", lambda e: e.activation(out=assq[:, 1:2], in_=assq[:, 1:2], func=AF.Exp, scale=-0.5),
                         reads=[assq.b], writes=[assq.b])
                    P.op("vector", lambda e: e.scalar_tensor_tensor(out=an_b[:], in0=arow[:], scalar=assq[:, 1:2], in1=anormb[:],
                                                                    op0=ALU.mult, op1=ALU.mult),
                         reads=[arow.b, assq.b, anormb.b], writes=[an_b.b])
                    at_ = at_s[s]
                    for g0 in range(0, 16, 8):
                        pt = ps[4 + (g0 // 8)]
                        ptb = pt[:].bitcast(BF16)

                        def tr(e, g0=g0, ptb=ptb):
                            ins = None
                            for j in range(8):
                                ins = e.transpose(ptb[:, j * 128:(j + 1) * 128], an_b[:, (g0 + j) * 128:(g0 + j + 1) * 128], ident_b[:])
                            return ins
                        P.op("tensor", tr, reads=[an_b.b, ident_b.b], writes=[pt.b])
                        P.op("vector", lambda e, g0=g0, ptb=ptb, at_=at_: e.tensor_copy(
                            out=at_[:, g0:g0 + 8, :].rearrange("p c t -> p (c t)"), in_=ptb[:, 0:1024]),
                            reads=[pt.b], writes=[at_.b])
                    P.dma("sync", [(MS[qb, :, 0:16, :], at_[:])], reads=[at_.b], sem_owner=at_.b)

            for qb in range(NB):
                attn_block(qb)
            P.barrier()
            P.emit()

        if stage == 5:
            return nc
        phE = contextlib.ExitStack()
        with phE:
            yacc = SB(P, "yacc2", [128, KC, T], F32, stack=phE)
            ht = SB(P, "ht2", [128, WK, T], BF16, stack=phE)
            mst = ht
            wblk, ring, wdv2 = make_wbufs(phE, "e")
            sgts = [sq[0], SB(P, "sg1e", [128, T], F32, stack=phE)]
            at = [SB(P, "at2%d" % i, [128, FBC, T], BF16, stack=phE) for i in range(2)]
            xgs = [SB(P, "xg2%d" % i, [128, XG, T], F32, stack=phE) for i in range(2)]
            ffn = make_ffn(ht, yacc, ring, wdv2, sgts, at)
            stream = make_stream(xgs)
            wov = w_out.rearrange("(mc p) d -> p mc d", p=128)
            yacc_store2 = Buf("yacc_store2")
            final_evs = []
            BPT = T // 128
            for lt in range(NT):
                tw = T
                x1d = Buf("x1d%d" % lt)
                P.dma("sync", [(mst[:, 0:MC, j * 128:(j + 1) * 128], MS[lt * BPT + j]) for j in range(BPT)],
                      writes=[mst.b], sem_owner=mst.b)
                for b0 in range(0, D, 256):
                    w_ = wblk[(b0 // 256) % 2]
                    P.dma("gpsimd", [(w_[:, 0:MC, :], WOS[b0 // 256])], writes=[w_.b], sem_owner=w_.b)
                    for j in range(2):
                        dc = b0 // 128 + j
                        pb = ps[dc % 2]

                        def mm(e, w_=w_, pb=pb, j=j):
                            ins = None
                            for mc in range(MC):
                                ins = e.matmul(pb[:, 0:tw], lhsT=w_[:, mc, j * 128:(j + 1) * 128], rhs=mst[:, mc, 0:tw],
                                               start=(mc == 0), stop=(mc == MC - 1))
                            return ins
                        P.op("tensor", mm, reads=[w_.b, mst.b], writes=[pb.b])
                        P.op("scalar", lambda e, dc=dc, pb=pb: e.activation(out=yacc[:, dc, 0:tw], in_=pb[:, 0:tw], func=AF.Copy),
                             reads=[pb.b], writes=[yacc.b])
                stats_resident(yacc, tw)
                stream(X1[lt], tw, lambda kc, xg, j: res_chunk(yacc[:, kc, 0:tw], yacc.b, xg[:, j, 0:tw], xg.b,
                                                             yacc[:, kc, 0:tw], yacc.b, kc, 1, 0, tw))
                step = max(1, KC // 2)
                P.dma("sync", [(X1[lt, :, k0:k0 + step, :], yacc[:, k0:k0 + step, :]) for k0 in range(0, KC, step)],
                      reads=[yacc.b], writes=[x1d], sem_owner=yacc_store2)
                stats_resident(yacc, tw)
                for kc in range(KC):
                    mod_chunk(yacc[:, kc, 0:tw], yacc.b, ht, kc, 2, 0, tw)
                ffn(1, tw)
                stats_resident(yacc, tw)

                def fin(kc, xg, j):
                    res_chunk(xg[:, j, 0:tw], xg.b, xg[:, j, 0:tw], xg.b, yacc[:, kc, 0:tw], yacc.b, kc, 2, 0, tw)
                    if j == XG - 1 or kc == KC - 1:
                        g0 = kc - j
                        ev = P.dma("sync", [(out[lt, :, g0:g0 + j + 1, :], xg[:, 0:j + 1, 0:tw])], reads=[xg.b], sem_owner=xg.b)
                        final_evs.append(ev)
                stream(X1[lt], tw, fin, dram_reads=[x1d])
            P.fence_events("sync", final_evs)
            P.barrier()
            P.emit()
    return nc


def feat_major(v):
    return np.ascontiguousarray(v.reshape(-1, 128).T)


def bcast_rows(v):
    return np.ascontiguousarray(np.broadcast_to(v.reshape(1, -1), (128, v.size))).astype(np.float32)


def const_tables(cfg):
    idx = np.arange(128)
    ident = np.eye(128, dtype=np.float32)
    Rm = np.zeros((128, 128), np.float32)
    for d in range(128):
        if (d % 64) < 32:
            Rm[d + 32, d] = -1.0
        else:
            Rm[d - 32, d] = 1.0
    tri_f = (idx[:, None] <= idx[None, :]).astype(np.float32)
    tri_b = (idx[:, None] >= idx[None, :]).astype(np.float32)
    neg_f = np.where(idx[:, None] <= idx[None, :], 0.0, -30000.0).astype(np.float32)
    neg_b = np.where(idx[:, None] >= idx[None, :], 0.0, -30000.0).astype(np.float32)
    consts = np.stack([ident, Rm, tri_f, tri_b, neg_f, neg_b, np.zeros((128, 128), np.float32)], axis=1)
    qi = idx[:, None]
    kj = np.arange(384)[None, :]
    amask = np.where(np.abs(kj - 128 - qi) <= 128, 0.0, -30000.0).astype(np.float32)
    rows = cfg.SEQ // cfg.GRID_W
    row = np.repeat(np.arange(rows), cfg.GRID_W).astype(np.float32)
    col = np.tile(np.arange(cfg.GRID_W), rows).astype(np.float32)
    inv = (np.float32(10000.0) ** (-np.arange(32, dtype=np.float32) / np.float32(32))).astype(np.float32)
    ar = row[:, None] * inv
    ac = col[:, None] * inv
    ang = np.concatenate([ar, ar, ac, ac], axis=-1)
    cosT = np.ascontiguousarray(np.cos(ang).T.astype(np.float32))
    sinT = np.ascontiguousarray(np.sin(ang).T.astype(np.float32))
    return dict(consts=np.ascontiguousarray(consts), amask=amask, cosT=cosT, sinT=sinT)


def make_inputs(cfg, b, tabs, x, c, ctx, c_ctx, w_mod, b_mod, norm_pre, norm_post, w_ffn_gate, w_ffn_up, w_ffn_down,
                w_in, attn_sink, attn_norm, conv_w, conv_b, a_log, dt_bias, d_skip, ssm_norm, w_out):
    KC, T = cfg.KC, cfg.T
    xin_c = np.ascontiguousarray(ctx[b].reshape(cfg.CTX, KC, 128).transpose(2, 1, 0))
    xin_l = np.ascontiguousarray(x[b].reshape(cfg.NT, T, KC, 128).transpose(0, 3, 2, 1))
    cc = np.stack([c[b], c_ctx], axis=-1)
    cT = np.ascontiguousarray(cc.reshape(KC, 128, 2).transpose(1, 0, 2))
    NXC = cfg.XBC // 128
    m = {
        "xin_c": xin_c, "xin_l": xin_l, "cT": cT, "w_mod": w_mod[0], "b_modT": feat_major(b_mod[0]),
        "npreT": feat_major(norm_pre[0].reshape(-1)), "npostT": feat_major(norm_post[0].reshape(-1)),
        "w_in": w_in[0], "w_out": w_out[0],
        "sinkb_d": bcast_rows(attn_sink[0]), "anormb_d": bcast_rows(attn_norm[0]), "snormb_d": bcast_rows(ssm_norm[0]),
        "dtbb_d": bcast_rows(dt_bias[0].reshape(-1)), "alogb_d": bcast_rows(a_log[0].reshape(-1)), "dskb_d": bcast_rows(d_skip[0]),
        "convw_d": np.ascontiguousarray(conv_w[0].reshape(5, NXC, 128).transpose(2, 1, 0)),
        "convb_d": feat_major(conv_b[0]),
    }
    m.update(tabs)
    for i in range(2):
        m["wg%d" % i] = w_ffn_gate[0, i]
        m["wu%d" % i] = w_ffn_up[0, i]
        m["wd%d" % i] = w_ffn_down[0, i]
    return m


def run(cfg, inputs, stage=99, ncores=2, trace=False):
    nc = build(cfg, stage)
    tabs = const_tables(cfg)
    maps = [make_inputs(cfg, b, tabs, **inputs) for b in range(ncores)]
    res = run_bass_kernel_spmd(nc, maps, core_ids=list(range(ncores)), trace=trace)
    outs = []
    for b in range(ncores):
        o = res.results[b]["out"]
        outs.append(o.transpose(0, 3, 2, 1).reshape(cfg.SEQ, cfg.D))
    return np.stack(outs, 0), res


def kernel(**inputs):
    cfg = Cfg()
    inputs = {k: np.asarray(v) for k, v in inputs.items()}
    o, _ = run(cfg, inputs)
    return np.ascontiguousarray(o.astype(np.float32))
```

```python
import contextlib
import os
MASK = int(os.environ.get('INPROJ_MASK', '127'))
SSDM = int(os.environ.get('SSD_MODE', '7'))
YSTEPS = int(os.environ.get('YSTEPS', '99'))
GP_LIM = int(os.environ.get('GP_LIM', '4'))
import numpy as np
import concourse.bass as bass
import concourse.mybir as mybir
from concourse.bass_utils import run_bass_kernel_spmd

F32 = mybir.dt.float32
BF16 = mybir.dt.bfloat16
ALU = mybir.AluOpType
AF = mybir.ActivationFunctionType
AX = mybir.AxisListType


class Cfg:
    def __init__(self, D=4096, SEQ=8192, CTX=256, DFF=11008, GRID_W=64):
        self.D = D
        self.KC = D // 128
        self.SEQ = SEQ
        self.CTX = CTX
        self.DFF = DFF
        self.FC = DFF // 128
        self.GRID_W = GRID_W
        self.T = 512
        assert CTX == 256 and SEQ % self.T == 0
        self.NT = SEQ // self.T
        self.NH = 16
        self.NKV = 4
        self.DH = 128
        self.AW = self.NH * self.DH
        self.SI = D // 2
        self.SH = self.SI // 64
        self.SG = 4
        self.HPG = self.SH // 4
        self.NS = 128
        self.KV = self.NKV * self.DH
        self.BC = self.SG * self.NS
        self.XBC = self.SI + 2 * self.BC
        self.DT = 2 * self.SH
        self.C0 = self.AW + self.SI
        self.INC = self.C0 + 2 * self.KV + self.XBC + self.DT
        self.MW = self.AW + self.SI
        self.EPS = 1e-6


class Buf:
    def __init__(self, name):
        self.name = name
        self.last_w = None
        self.readers = {}
        self.dsem = None
        self.dcnt = 0


class Prog:
    def __init__(self, nc, stack):
        self.nc = nc
        self.stack = stack
        self.lists = {k: [] for k in ("sync", "scalar", "gpsimd", "vector", "tensor")}
        self.sems = {}
        self.cnt = {k: 0 for k in self.lists}
        self.waited = {k: {} for k in self.lists}
        for k in self.lists:
            self.sems[k] = stack.enter_context(nc.semaphore("e_" + k))
        self.nsem = 5
        self.store_events = {}
        self.dma_bufs = []
        self.gp_fifo = []

    def new_sem(self, name):
        self.nsem += 1
        return self.stack.enter_context(self.nc.semaphore(name))

    def _wait(self, q, ev):
        if ev is None:
            return
        sem, val = ev
        if sem is self.sems.get(q):
            if q == "tensor" or val <= self.cnt[q] - 8:
                return
        key = id(sem)
        if self.waited[q].get(key, 0) >= val:
            return
        self.waited[q][key] = val
        self.lists[q].append(lambda e, sem=sem, val=val: e.wait_ge(sem, val))

    def _deps(self, q, reads, writes):
        for b in reads:
            self._wait(q, b.last_w)
        for b in writes:
            self._wait(q, b.last_w)
            for sem_val in b.readers.values():
                self._wait(q, sem_val)

    @staticmethod
    def _add_reader(b, ev):
        key = id(ev[0])
        if key not in b.readers or b.readers[key][1] < ev[1]:
            b.readers[key] = ev

    def op(self, q, fn, reads=(), writes=()):
        self._deps(q, reads, writes)
        self.cnt[q] += 1
        sem = self.sems[q]
        self.lists[q].append(lambda e, fn=fn, sem=sem: fn(e).then_inc(sem, 1))
        ev = (sem, self.cnt[q])
        for b in reads:
            self._add_reader(b, ev)
        for b in writes:
            b.last_w = ev
            b.readers = {}
        return ev

    def dma(self, q, outs_ins, reads=(), writes=(), sem_owner=None):
        self._deps(q, reads, writes)
        ow = sem_owner
        if ow.dsem is None:
            ow.dsem = self.new_sem("d_" + ow.name)
            self.dma_bufs.append(ow)
        for n_, (o, i) in enumerate(outs_ins):
            if q == "gpsimd":
                if len(self.gp_fifo) >= GP_LIM:
                    s_, v_ = self.gp_fifo.pop(0)
                    self.lists[q].append(lambda e, s_=s_, v_=v_: e.wait_ge(s_, v_))
            ow.dcnt += 16
            self.lists[q].append(lambda e, o=o, i=i, s=ow.dsem: e.dma_start(out=o, in_=i).then_inc(s, 16))
            if q == "gpsimd":
                self.gp_fifo.append((ow.dsem, ow.dcnt))
        ev = (ow.dsem, ow.dcnt)
        for b in reads:
            self._add_reader(b, ev)
        for b in writes:
            b.last_w = ev
            b.readers = {}
        return ev

    def fence_events(self, q, evs):
        for ev in evs:
            self._wait(q, ev)

    def barrier(self):
        evs = [(self.sems[k], self.cnt[k]) for k in self.lists if self.cnt[k] > 0]
        evs += [(b.dsem, b.dcnt) for b in self.dma_bufs if b.dsem is not None and b.dcnt > 0]
        for q in self.lists:
            for ev in evs:
                if ev[0] is self.sems[q]:
                    continue
                self._wait(q, ev)

    def emit(self):
        nc = self.nc
        lists = self.lists
        self.lists = {k: [] for k in lists}
        self._emit(lists)

    def _emit(self, L):
        nc = self.nc
        with nc.Block() as block:
            @block.sync
            def _(e):
                for th in L["sync"]:
                    th(e)

            @block.scalar
            def _(e):
                for th in L["scalar"]:
                    th(e)

            @block.gpsimd
            def _(e):
                for th in L["gpsimd"]:
                    th(e)

            @block.vector
            def _(e):
                for th in L["vector"]:
                    th(e)

            @block.tensor
            def _(e):
                for th in L["tensor"]:
                    th(e)


class SB:
    def __init__(self, P, name, shape, dtype, psum=False, stack=None):
        st = stack if stack is not None else P.stack
        if psum:
            self.t = st.enter_context(P.nc.psum_tensor(name, shape, dtype))
        else:
            self.t = st.enter_context(P.nc.sbuf_tensor(name, shape, dtype))
        self.b = Buf(name)

    def __getitem__(self, k):
        return self.t[k]


class V:
    def __init__(self, ap, buf):
        self.t = ap
        self.b = buf

    def __getitem__(self, k):
        return self.t[k]


def build(cfg, stage=99):
    c = cfg
    D, KC, T, FC = c.D, c.KC, c.T, c.FC
    SEQ, CTX, SI, SH, HPG, DT = c.SEQ, c.CTX, c.SI, c.SH, c.HPG, c.DT
    SC = SI // 128
    NXC = c.XBC // 128
    MC = 16 + SC
    WK = max(KC, MC)
    NB = SEQ // 128
    NT = c.NT
    NTT = NT + 1
    NTOK = CTX + SEQ
    XG = 2
    nc = bass.Bass("TRN2", target_bir_lowering=False)

    def din(name, shape, dt=F32):
        return nc.dram_tensor(name, list(shape), dt, kind="ExternalInput").ap()

    def dscr(name, shape, dt=F32):
        return nc.dram_tensor(name, list(shape), dt).ap()

    xin_c = din("xin_c", [128, KC, CTX])
    xin_l = din("xin_l", [NT, 128, KC, T])
    cT = din("cT", [128, KC, 2])
    w_mod = din("w_mod", [D, 9 * D])
    b_modT = din("b_modT", [128, 9 * KC])
    npreT = din("npreT", [128, 3 * KC])
    npostT = din("npostT", [128, 3 * KC])
    wg = [din("wg%d" % i, [D, c.DFF]) for i in range(2)]
    wu = [din("wu%d" % i, [D, c.DFF]) for i in range(2)]
    wd = [din("wd%d" % i, [c.DFF, D]) for i in range(2)]
    w_in = din("w_in", [D, c.INC])
    w_out = din("w_out", [c.MW, D])
    cosT = din("cosT", [128, SEQ])
    sinT = din("sinT", [128, SEQ])
    consts = din("consts", [128, 7, 128])
    amask = din("amask", [128, 384])
    sinkb_d = din("sinkb_d", [128, 16])
    anormb_d = din("anormb_d", [128, c.AW])
    snormb_d = din("snormb_d", [128, SI])
    dtbb_d = din("dtbb_d", [128, DT])
    alogb_d = din("alogb_d", [128, DT])
    dskb_d = din("dskb_d", [128, SH])
    convw_d = din("convw_d", [128, NXC, 5])
    convb_d = din("convb_d", [128, NXC])
    out = nc.dram_tensor("out", [NT, 128, KC, T], F32, kind="ExternalOutput").ap()

    X1 = dscr("X1", [NT, 128, KC, T])
    QT = dscr("QT", [16, 128, SEQ], BF16)
    KT = dscr("KT", [4, 128, SEQ], BF16)
    KCT = dscr("KCT", [4, 128, CTX], BF16)
    VV = dscr("VV", [NTOK, 512], BF16)
    ZS = dscr("ZS", [SEQ, SI])
    XW = (CTX + 4) + (SEQ + 4)
    XBCA = dscr("XBCA", [NXC, 128, XW])
    DTA = dscr("DTA", [NTOK, DT])
    XS = dscr("XS", [NTOK, SI])
    BTS = dscr("BTS", [NTOK, 512], BF16)
    BFS = dscr("BFS", [4, 128, NTOK])
    CFS = dscr("CFS", [4, 128, NTOK])
    YA = dscr("YA", [SEQ, SI])
    MS = dscr("MS", [NB, 128, MC, 128], BF16)
    NFB = (FC + 1) // 2
    WGS = [dscr("WGS%d" % i, [NFB, 128, KC, 256], BF16) for i in range(2)]
    WUS = [dscr("WUS%d" % i, [NFB, 128, KC, 256], BF16) for i in range(2)]
    WDS = [dscr("WDS%d" % i, [NFB, 128, 2, D], BF16) for i in range(2)]

    stack = contextlib.ExitStack()
    with stack:
        stack.enter_context(nc.allow_low_precision("bf16 matmul operands, fp32 PSUM accumulation"))
        P = Prog(nc, stack)

        ones_f = SB(P, "ones_f", [128, 128], F32)
        epsb = SB(P, "epsb", [128, 1], F32)
        cst = SB(P, "cst", [128, 7, 128], F32)
        ident_b = SB(P, "ident_b", [128, 128], BF16)
        modT = SB(P, "modT", [128, 9 * KC, 2], F32)
        gs = SB(P, "gs", [128, 3 * KC, 2], F32)
        gp = SB(P, "gp", [128, 3 * KC, 2], F32)
        bm = SB(P, "bm", [128, 9 * KC], F32)
        npre = SB(P, "npre", [128, 3 * KC], F32)
        npost = SB(P, "npost", [128, 3 * KC], F32)
        cs_f = SB(P, "cs_f", [128, KC, 2], F32)
        cs_b = SB(P, "cs_b", [128, KC, 2], BF16)
        rstd = SB(P, "rstd", [128, T], F32)
        sq = [SB(P, "sq%d" % i, [128, T], F32) for i in range(1)]
        tmpf = [SB(P, "tmpf%d" % i, [128, T], F32) for i in range(2)]
        ps = [SB(P, "ps%d" % i, [128, 512], F32, psum=True) for i in range(8)]
        ident_f = cst[:, 0, :]
        Rm = cst[:, 1, :]
        tri = [cst[:, 2, :], cst[:, 3, :]]
        negm_ = [cst[:, 4, :], cst[:, 5, :]]

        P.op("vector", lambda e: e.memset(ones_f[:], 1.0), writes=[ones_f.b])
        P.op("vector", lambda e: e.memset(epsb[:], c.EPS), writes=[epsb.b])
        P.dma("sync", [(cst[:], consts[:, :, :])], writes=[cst.b], sem_owner=cst.b)
        P.op("vector", lambda e: e.tensor_copy(out=ident_b[:], in_=cst[:, 0, :]), reads=[cst.b], writes=[ident_b.b])

        FBC = 2

        def load_w(dst, src_view, rows, cols0, ncols):
            pairs = []
            step = max(1, min(8, (2 * 1024 * 1024) // (128 * ncols * 4)))
            for k0 in range(0, rows, step):
                k1 = min(rows, k0 + step)
                pairs.append((dst[:, k0:k1, 0:ncols], src_view[:, k0:k1, cols0:cols0 + ncols]))
            P.dma("gpsimd", pairs, writes=[dst.b], sem_owner=dst.b)

        def sumsq(chunk_ap, chunk_buf, idx, n, psb, tw):
            s_ = sq[0]
            P.op("scalar", lambda e: e.activation(out=s_[:, 0:tw], in_=chunk_ap, func=AF.Square),
                 reads=[chunk_buf], writes=[s_.b])
            P.op("tensor", lambda e: e.matmul(psb[:, 0:tw], lhsT=ones_f[:], rhs=s_[:, 0:tw],
                                              start=(idx == 0), stop=(idx == n - 1)),
                 reads=[s_.b, ones_f.b], writes=[psb.b])

        def finish_rstd(psb, tw, dim):
            P.op("scalar", lambda e: e.activation(out=rstd[:, 0:tw], in_=psb[:, 0:tw], func=AF.Ln, bias=epsb[:, 0:1], scale=1.0 / dim),
                 reads=[psb.b, epsb.b], writes=[rstd.b])
            P.op("scalar", lambda e: e.activation(out=rstd[:, 0:tw], in_=rstd[:, 0:tw], func=AF.Exp, scale=-0.5),
                 reads=[rstd.b], writes=[rstd.b])

        def stats_resident(src, tw):
            for kc in range(KC):
                sumsq(src[:, kc, 0:tw], src.b, kc, KC, ps[7], tw)
            finish_rstd(ps[7], tw, D)

        def mod_chunk(src_ap, src_buf, dst, kc, slot, r, tw):
            t_ = tmpf[kc % 2]
            P.op("vector", lambda e: e.scalar_tensor_tensor(
                out=t_[:, 0:tw], in0=src_ap, scalar=gs[:, slot * KC + kc, r:r + 1], in1=rstd[:, 0:tw],
                op0=ALU.mult, op1=ALU.mult), reads=[src_buf, gs.b, rstd.b], writes=[t_.b])
            P.op("scalar", lambda e: e.activation(
                out=dst[:, kc, 0:tw], in_=t_[:, 0:tw], func=AF.Identity,
                bias=modT[:, (3 * slot) * KC + kc, r:r + 1], scale=1.0),
                reads=[t_.b, modT.b], writes=[dst.b])

        def res_chunk(dst_ap, dst_buf, x_ap, x_buf, y_ap, y_buf, kc, slot, r, tw):
            t_ = tmpf[kc % 2]
            P.op("vector", lambda e: e.scalar_tensor_tensor(
                out=t_[:, 0:tw], in0=y_ap, scalar=gp[:, slot * KC + kc, r:r + 1], in1=rstd[:, 0:tw],
                op0=ALU.mult, op1=ALU.mult), reads=[y_buf, gp.b, rstd.b], writes=[t_.b])
            P.op("vector", lambda e: e.tensor_tensor(out=dst_ap, in0=x_ap, in1=t_[:, 0:tw], op=ALU.add),
                 reads=[t_.b, x_buf], writes=[dst_buf])

        def make_wbufs(stk, tag):
            WF = max(WK * 256, 2 * D)
            Wt = [SB(P, "%sW%d" % (tag, i), [128, WF], BF16, stack=stk) for i in range(4)]
            wblk = [V(Wt[i][:, 0:WK * 256].rearrange("p (kc n) -> p kc n", n=256), Wt[i].b) for i in range(4)]
            Hh = KC // 2
            ring = []
            for i in range(2):
                for h in range(2):
                    ring.append(V(Wt[i][:, h * Hh * 256:(h + 1) * Hh * 256].rearrange("p (kc n) -> p kc n", n=256), Wt[i].b))
            wdv_ = [V(Wt[2 + i][:, 0:2 * D].rearrange("p (a d) -> p a d", d=D), Wt[2 + i].b) for i in range(2)]
            return wblk, ring, wdv_

        def make_ffn(ht, yacc, ring, wdv_, sgts, at):
            Hh = KC // 2

            def ffn(i, tw):
                rctr = [0]

                def emit_D(pb_, dc):
                    a_ = at[pb_ % 2]
                    wd_ = wdv_[pb_ % 2]
                    nchp = min(FBC, FC - pb_ * FBC)
                    py = ps[4 + (dc % 3)]

                    def mmd(e):
                        ins = None
                        for j in range(nchp):
                            ins = e.matmul(py[:, 0:tw], lhsT=wd_[:, j, dc * 128:(dc + 1) * 128], rhs=a_[:, j, 0:tw],
                                           start=(j == 0), stop=(j == nchp - 1))
                        return ins
                    P.op("tensor", mmd, reads=[wd_.b, a_.b], writes=[py.b])
                    if pb_ == 0:
                        P.op("vector", lambda e: e.tensor_copy(out=yacc[:, dc, 0:tw], in_=py[:, 0:tw]),
                             reads=[py.b], writes=[yacc.b])
                    else:
                        P.op("vector", lambda e: e.tensor_tensor(
                            out=yacc[:, dc, 0:tw], in0=yacc[:, dc, 0:tw], in1=py[:, 0:tw], op=ALU.add),
                            reads=[py.b, yacc.b], writes=[yacc.b])

                for fb in range(NFB):
                    nch = min(FBC, FC - fb * FBC)
                    nco = nch * 128
                    hv = []
                    for srcd in (WGS[i], WUS[i]):
                        for h in range(2):
                            rb = ring[rctr[0] % 4]
                            rctr[0] += 1
                            P.dma("sync", [(rb[:, 0:Hh, 0:nco], srcd[fb, :, h * Hh:(h + 1) * Hh, 0:nco])],
                                  writes=[rb.b], sem_owner=rb.b)
                            hv.append(rb)
                    wd_ = wdv_[fb % 2]
                    P.dma("sync", [(wd_[:, 0:nch, :], WDS[i][fb, :, 0:nch, :])], writes=[wd_.b], sem_owner=wd_.b)
                    pgs = [ps[0], ps[1]]
                    pus = [ps[2], ps[3]]
                    slot = 0
                    for mi, (banks, halves) in enumerate(((pgs, hv[0:2]), (pus, hv[2:4]))):
                        for h in range(2):
                            rb = halves[h]
                            for j in range(nch):
                                pb = banks[j]
                                for k0 in range(0, Hh, 4):
                                    def mm(e, rb=rb, pb=pb, j=j, k0=k0, h=h):
                                        ins = None
                                        for kk in range(k0, min(Hh, k0 + 4)):
                                            kc = h * Hh + kk
                                            ins = e.matmul(pb[:, 0:tw], lhsT=rb[:, kk, j * 128:(j + 1) * 128], rhs=ht[:, kc, 0:tw],
                                                           start=(kc == 0), stop=(kc == KC - 1))
                                        return ins
                                    P.op("tensor", mm, reads=[rb.b, ht.b], writes=[pb.b])
                                    if fb >= 1 and slot < KC:
                                        emit_D(fb - 1, slot)
                                        slot += 1
                        if mi == 0:
                            for j in range(nch):
                                P.op("scalar", lambda e, j=j: e.activation(out=sgts[j][:, 0:tw], in_=pgs[j][:, 0:tw], func=AF.Silu),
                                     reads=[pgs[j].b], writes=[sgts[j].b])
                    if fb >= 1:
                        while slot < KC:
                            emit_D(fb - 1, slot)
                            slot += 1
                    a_ = at[fb % 2]
                    for j in range(nch):
                        P.op("vector", lambda e, j=j, a_=a_: e.tensor_tensor(
                            out=a_[:, j, 0:tw], in0=sgts[j][:, 0:tw], in1=pus[j][:, 0:tw], op=ALU.mult),
                            reads=[sgts[j].b, pus[j].b], writes=[a_.b])
                for dc in range(KC):
                    emit_D(NFB - 1, dc)
            return ffn

        def make_stream(xgs):
            ctr = [0]

            def stream(src, tw, body, dram_reads=()):
                for g0 in range(0, KC, XG):
                    n = min(XG, KC - g0)
                    xg = xgs[ctr[0] % 2]
                    ctr[0] += 1
                    P.dma("sync", [(xg[:, 0:n, 0:tw], src[:, g0:g0 + n, 0:tw])], reads=list(dram_reads), writes=[xg.b], sem_owner=xg.b)
                    for j in range(n):
                        body(g0 + j, xg, j)
            return stream

        phA = contextlib.ExitStack()
        with phA:
            yacc = SB(P, "yacc", [128, KC, T], F32, stack=phA)
            ht = SB(P, "ht", [128, KC, T], BF16, stack=phA)
            wblk, ring, wdv2 = make_wbufs(phA, "a")
            sgts = [sq[0], SB(P, "sg1", [128, T], F32, stack=phA)]
            at = [SB(P, "at%d" % i, [128, FBC, T], BF16, stack=phA) for i in range(2)]
            xgs = [SB(P, "xg%d" % i, [128, XG, T], F32, stack=phA) for i in range(2)]
            stgf = [V(xgs[i][:, 0, :], xgs[i].b) for i in range(2)]
            stgb = [V(at[i][:, 0, :], at[i].b) for i in range(2)]
            qf = [sq[0]]
            cos_t = V(xgs[0][:, 1, :], xgs[0].b)
            sin_t = V(xgs[1][:, 1, :], xgs[1].b)
            dtbb = SB(P, "dtbb", [128, DT], F32, stack=phA)
            zt = SB(P, "zt", [128, NXC, 2], F32, stack=phA)
            sp = [SB(P, "sp%d" % i, [128, DT], F32, stack=phA) for i in range(4)]
            ffn = make_ffn(ht, yacc, ring, wdv2, sgts, at)
            stream = make_stream(xgs)

            P.dma("sync", [(cs_f[:], cT[:, :, :])], writes=[cs_f.b], sem_owner=cs_f.b)
            P.dma("sync", [(bm[:], b_modT[:, :])], writes=[bm.b], sem_owner=bm.b)
            P.dma("sync", [(npre[:], npreT[:, :])], writes=[npre.b], sem_owner=npre.b)
            P.dma("sync", [(npost[:], npostT[:, :])], writes=[npost.b], sem_owner=npost.b)
            P.dma("sync", [(dtbb[:], dtbb_d[:, :])], writes=[dtbb.b], sem_owner=dtbb.b)
            P.op("scalar", lambda e: e.activation(out=cs_b[:], in_=cs_f[:], func=AF.Silu),
                 reads=[cs_f.b], writes=[cs_b.b])
            wmv = w_mod.rearrange("(kc p) n -> p kc n", p=128)
            nblk = 9 * KC // FBC
            for blk in range(nblk):
                wb = wblk[blk % 2]
                load_w(wb, wmv, KC, blk * 256, 256)
                pb = ps[blk % 2]
                for j in range(FBC):
                    fcidx = blk * FBC + j

                    def mm(e, wb=wb, pb=pb, j=j):
                        ins = None
                        for kc in range(KC):
                            ins = e.matmul(pb[:, j * 2:j * 2 + 2], lhsT=wb[:, kc, j * 128:(j + 1) * 128],
                                           rhs=cs_b[:, kc, :], start=(kc == 0), stop=(kc == KC - 1))
                        return ins
                    P.op("tensor", mm, reads=[wb.b, cs_b.b], writes=[pb.b])
                    P.op("vector", lambda e, pb=pb, j=j, fcidx=fcidx: e.tensor_scalar(
                        out=modT[:, fcidx, :], in0=pb[:, j * 2:j * 2 + 2], scalar1=bm[:, fcidx:fcidx + 1],
                        scalar2=None, op0=ALU.add), reads=[pb.b, bm.b], writes=[modT.b])
            for s in range(3):
                coef = 1.0 if s == 1 else 0.5
                for r in range(2):
                    P.op("vector", lambda e, s=s, r=r: e.scalar_tensor_tensor(
                        out=gs[:, s * KC:(s + 1) * KC, r], in0=modT[:, (3 * s + 1) * KC:(3 * s + 2) * KC, r], scalar=1.0,
                        in1=npre[:, s * KC:(s + 1) * KC], op0=ALU.add, op1=ALU.mult),
                        reads=[modT.b, npre.b], writes=[gs.b])
                    P.op("vector", lambda e, s=s, r=r, coef=coef: e.scalar_tensor_tensor(
                        out=gp[:, s * KC:(s + 1) * KC, r], in0=modT[:, (3 * s + 2) * KC:(3 * s + 3) * KC, r], scalar=coef,
                        in1=npost[:, s * KC:(s + 1) * KC], op0=ALU.mult, op1=ALU.mult),
                        reads=[modT.b, npost.b], writes=[gp.b])

            P.op("vector", lambda e: e.memset(zt[:], 0.0), writes=[zt.b])
            xv = XBCA.rearrange("c p t -> p c t")
            for c0 in (0, CTX + 2, CTX + 4, CTX + 4 + SEQ + 2):
                P.dma("sync", [(xv[:, x0:min(NXC, x0 + 4), c0:c0 + 2], zt[:, x0:min(NXC, x0 + 4), :]) for x0 in range(0, NXC, 4)],
                      reads=[zt.b], sem_owner=zt.b)

            wst = [Buf("wst%d" % k) for k in range(5)]
            pslots = [wblk[0], wblk[1], wblk[0], wblk[1]]
            pk = 0
            for i in range(2):
                wgv = wg[i].rearrange("(kc p) f -> p kc f", p=128)
                wuv = wu[i].rearrange("(kc p) f -> p kc f", p=128)
                wdv = wd[i].rearrange("(fc p) d -> p fc d", p=128)
                for fb in range(NFB):
                    nch = min(FBC, FC - fb * FBC)
                    nco = nch * 128
                    for (srcv, dstd) in ((wgv, WGS[i]), (wuv, WUS[i])):
                        sl = pslots[pk % 4]
                        load_w(sl, srcv, KC, fb * FBC * 128, nco)
                        P.dma("sync", [(dstd[fb, :, :, 0:nco], sl[:, 0:KC, 0:nco])], reads=[sl.b], sem_owner=wst[pk % 4])
                        pk += 1
                    wdb = wdv2[fb % 2]
                    P.dma("gpsimd", [(wdb[:, j, :], wdv[:, fb * FBC + j, :]) for j in range(nch)],
                          writes=[wdb.b], sem_owner=wdb.b)
                    P.dma("sync", [(WDS[i][fb, :, 0:nch, :], wdb[:, 0:nch, :])], reads=[wdb.b], sem_owner=wst[4])
            P.barrier()

            wiv = w_in.rearrange("(kc p) n -> p kc n", p=128)
            wslots = [wblk[0], wblk[1], wblk[2], wblk[3]]
            wctr = [0]
            sctr = [0, 0]

            def next_w():
                w_ = wslots[wctr[0] % 4]
                wctr[0] += 1
                return w_

            def fm_group(col0, ncols, evac, tw):
                for b0 in range(0, ncols, 256):
                    nb_ = min(256, ncols - b0)
                    w_ = next_w()
                    load_w(w_, wiv, KC, col0 + b0, nb_)
                    for j in range(nb_ // 128):
                        pb = ps[(b0 // 128 + j) % 2]

                        def mm(e, w_=w_, pb=pb, j=j):
                            ins = None
                            for kc in range(KC):
                                ins = e.matmul(pb[:, 0:tw], lhsT=w_[:, kc, j * 128:(j + 1) * 128], rhs=ht[:, kc, 0:tw],
                                               start=(kc == 0), stop=(kc == KC - 1))
                            return ins
                        P.op("tensor", mm, reads=[w_.b, ht.b], writes=[pb.b])
                        evac(b0 // 128 + j, pb)

            def tm_group(col0, ncols, evac, tw):
                for b0 in range(0, ncols, 256):
                    nb_ = min(256, ncols - b0)
                    w_ = next_w()
                    load_w(w_, wiv, KC, col0 + b0, nb_)
                    for tg in range(tw // 128):
                        pb = ps[2 + (tg % 2)]

                        def mm(e, w_=w_, pb=pb, tg=tg, nb_=nb_):
                            ins = None
                            for kc in range(KC):
                                ins = e.matmul(pb[:, 0:nb_], lhsT=ht[:, kc, tg * 128:(tg + 1) * 128], rhs=w_[:, kc, 0:nb_],
                                               start=(kc == 0), stop=(kc == KC - 1))
                            return ins
                        P.op("tensor", mm, reads=[w_.b, ht.b], writes=[pb.b])
                        evac(tg, b0, nb_, pb)

            def nstg(kind):
                i = sctr[kind] % 2
                sctr[kind] += 1
                return (stgf if kind == 0 else stgb)[i]

            def inproj(ti, tw):
                is_ctx = (ti == 0)
                t0 = (ti - 1) * T
                if not is_ctx:
                    P.dma("sync", [(cos_t[:, 0:tw], cosT[:, t0:t0 + tw])], writes=[cos_t.b], sem_owner=cos_t.b)
                    P.dma("sync", [(sin_t[:, 0:tw], sinT[:, t0:t0 + tw])], writes=[sin_t.b], sem_owner=sin_t.b)

                def rope_evac(dst_fn):
                    def ev(ci, pb):
                        q_ = qf[0]
                        P.op("scalar", lambda e: e.activation(out=q_[:, 0:tw], in_=pb[:, 0:tw], func=AF.Copy),
                             reads=[pb.b], writes=[q_.b])
                        pr = ps[4 + (ci % 2)]
                        P.op("tensor", lambda e: e.matmul(pr[:, 0:tw], lhsT=Rm, rhs=q_[:, 0:tw], start=True, stop=True),
                             reads=[q_.b, cst.b], writes=[pr.b])
                        t1 = tmpf[0]
                        t2 = tmpf[1]
                        P.op("vector", lambda e: e.tensor_tensor(out=t1[:, 0:tw], in0=q_[:, 0:tw], in1=cos_t[:, 0:tw], op=ALU.mult),
                             reads=[q_.b, cos_t.b], writes=[t1.b])
                        P.op("vector", lambda e: e.tensor_tensor(out=t2[:, 0:tw], in0=pr[:, 0:tw], in1=sin_t[:, 0:tw], op=ALU.mult),
                             reads=[pr.b, sin_t.b], writes=[t2.b])
                        sb_ = nstg(1)
                        P.op("vector", lambda e: e.tensor_tensor(out=sb_[:, 0:tw], in0=t1[:, 0:tw], in1=t2[:, 0:tw], op=ALU.add),
                             reads=[t1.b, t2.b], writes=[sb_.b])
                        P.dma("sync", dst_fn(ci, sb_), reads=[sb_.b], sem_owner=sb_.b)
                    return ev

                if not is_ctx:
                    def qdst(ci, sb_):
                        return [(QT[ci, :, t0:t0 + tw], sb_[:, 0:tw])]
                    fm_group(0, c.AW, rope_evac(qdst), tw)

                    def zev(tg, b0, nb_, pb):
                        sf = nstg(0)
                        P.op("scalar", lambda e: e.activation(out=sf[:, 0:nb_], in_=pb[:, 0:nb_], func=AF.Silu),
                             reads=[pb.b], writes=[sf.b])
                        r0 = t0 + tg * 128
                        P.dma("sync", [(ZS[r0:r0 + 128, b0:b0 + nb_], sf[:, 0:nb_])], reads=[sf.b], sem_owner=sf.b)
                    tm_group(c.AW, SI, zev, tw)

                    def kdst(ci, sb_):
                        return [(KT[ci, :, t0:t0 + tw], sb_[:, 0:tw])]
                    fm_group(c.C0, c.KV, rope_evac(kdst), tw)
                else:
                    def kcev(ci, pb):
                        sb_ = nstg(1)
                        P.op("scalar", lambda e: e.activation(out=sb_[:, 0:tw], in_=pb[:, 0:tw], func=AF.Copy),
                             reads=[pb.b], writes=[sb_.b])
                        P.dma("sync", [(KCT[ci, :, :], sb_[:, 0:tw])], reads=[sb_.b], sem_owner=sb_.b)
                    fm_group(c.C0, c.KV, kcev, tw)
                vrow0 = 0 if is_ctx else CTX + t0

                def vev(tg, b0, nb_, pb):
                    sb_ = nstg(1)
                    P.op("vector", lambda e: e.tensor_copy(out=sb_[:, 0:nb_], in_=pb[:, 0:nb_]),
                         reads=[pb.b], writes=[sb_.b])
                    r0 = vrow0 + tg * 128
                    P.dma("sync", [(VV[r0:r0 + 128, b0:b0 + nb_], sb_[:, 0:nb_])], reads=[sb_.b], sem_owner=sb_.b)
                tm_group(c.C0 + c.KV, c.KV, vev, tw)
                xcol0 = 2 if is_ctx else (CTX + 4 + 2 + t0)

                def xev(ci, pb):
                    sf = nstg(0)
                    P.op("scalar", lambda e: e.activation(out=sf[:, 0:tw], in_=pb[:, 0:tw], func=AF.Copy),
                         reads=[pb.b], writes=[sf.b])
                    P.dma("sync", [(XBCA[ci, :, xcol0:xcol0 + tw], sf[:, 0:tw])], reads=[sf.b], sem_owner=sf.b)
                fm_group(c.C0 + 2 * c.KV, c.XBC, xev, tw)

                def dtev(tg, b0, nb_, pb):
                    x_, a_, m_, o_ = sp
                    P.op("vector", lambda e: e.tensor_tensor(out=x_[:], in0=pb[:, 0:DT], in1=dtbb[:], op=ALU.add),
                         reads=[pb.b, dtbb.b], writes=[x_.b])
                    P.op("scalar", lambda e: e.activation(out=a_[:], in_=x_[:], func=AF.Abs),
                         reads=[x_.b], writes=[a_.b])
                    P.op("scalar", lambda e: e.activation(out=a_[:], in_=a_[:], func=AF.Exp, scale=-1.0),
                         reads=[a_.b], writes=[a_.b])
                    P.op("scalar", lambda e: e.activation(out=a_[:], in_=a_[:], func=AF.Ln, bias=ones_f[:, 0:1], scale=1.0),
                         reads=[a_.b, ones_f.b], writes=[a_.b])
                    P.op("vector", lambda e: e.tensor_scalar_max(out=m_[:], in0=x_[:], scalar1=0.0),
                         reads=[x_.b], writes=[m_.b])
                    P.op("vector", lambda e: e.tensor_tensor(out=o_[:], in0=m_[:], in1=a_[:], op=ALU.add),
                         reads=[m_.b, a_.b], writes=[o_.b])
                    r0 = vrow0 + tg * 128
                    P.dma("sync", [(DTA[r0:r0 + 128, :], o_[:, :])], reads=[o_.b], sem_owner=o_.b)
                tm_group(c.C0 + 2 * c.KV + c.XBC, DT, dtev, tw)

            yacc_store = Buf("yacc_store")
            for ti in range(NTT):
                is_ctx = (ti == 0)
                r = 1 if is_ctx else 0
                tw = CTX if is_ctx else T
                src = xin_c if is_ctx else xin_l[ti - 1]
                stream(src, tw, lambda kc, xg, j: sumsq(xg[:, j, 0:tw], xg.b, kc, KC, ps[7], tw))
                finish_rstd(ps[7], tw, D)
                stream(src, tw, lambda kc, xg, j: mod_chunk(xg[:, j, 0:tw], xg.b, ht, kc, 0, r, tw))
                ffn(0, tw)
                stats_resident(yacc, tw)
                stream(src, tw, lambda kc, xg, j: res_chunk(yacc[:, kc, 0:tw], yacc.b, xg[:, j, 0:tw], xg.b,
                                                           yacc[:, kc, 0:tw], yacc.b, kc, 0, r, tw))
                if not is_ctx:
                    step = max(1, KC // 2)
                    P.dma("sync", [(X1[ti - 1, :, k0:k0 + step, :], yacc[:, k0:k0 + step, :]) for k0 in range(0, KC, step)],
                          reads=[yacc.b], sem_owner=yacc_store)
                stats_resident(yacc, tw)
                for kc in range(KC):
                    mod_chunk(yacc[:, kc, 0:tw], yacc.b, ht, kc, 1, r, tw)
                if stage >= 2:
                    inproj(ti, tw)
            P.barrier()
            P.emit()
        if stage <= 2:
            return nc

        NCH_ALL = NTOK // 128
        phB = contextlib.ExitStack()
        with phB:
            convw = SB(P, "convw", [128, NXC, 5], F32, stack=phB)
            convb = SB(P, "convb", [128, NXC], F32, stack=phB)
            win = [SB(P, "win%d" % i, [128, NXC, 132], F32, stack=phB) for i in range(2)]
            acc = [SB(P, "acc%d" % i, [128, 128], F32, stack=phB) for i in range(2)]
            cx = [SB(P, "cx%d" % i, [128, SC + 4, 128], F32, stack=phB) for i in range(2)]
            bfc = [SB(P, "bfc%d" % i, [128, 8, 128], F32, stack=phB) for i in range(2)]
            xs_s = [SB(P, "xs_s%d" % i, [128, SI], F32, stack=phB) for i in range(2)]
            bt_s = [SB(P, "bt_s%d" % i, [128, 512], BF16, stack=phB) for i in range(2)]
            P.dma("sync", [(convw[:], convw_d[:, :, :])], writes=[convw.b], sem_owner=convw.b)
            P.dma("sync", [(convb[:], convb_d[:, :])], writes=[convb.b], sem_owner=convb.b)
            xv = XBCA.rearrange("c p t -> p c t")
            def conv_chunk(ch):
                    s = ch % 2
                    if ch < CTX // 128:
                        base = ch * 128
                        tok0 = ch * 128
                    else:
                        base = CTX + 4 + (ch - CTX // 128) * 128
                        tok0 = ch * 128
                    w_ = win[s]
                    P.dma("sync", [(w_[:], xv[:, :, base:base + 132])], writes=[w_.b], sem_owner=w_.b)
                    cx_, bfc_ = cx[s], bfc[s]
                    for xc in range(NXC):
                        a_ = acc[xc % 2]
                        P.op("vector", lambda e, a_=a_, xc=xc, w_=w_: e.tensor_scalar(
                            out=a_[:], in0=w_[:, xc, 0:128], scalar1=convw[:, xc, 0:1], scalar2=None, op0=ALU.mult),
                            reads=[w_.b, convw.b], writes=[a_.b])
                        for k in range(1, 5):
                            P.op("vector", lambda e, a_=a_, xc=xc, w_=w_, k=k: e.scalar_tensor_tensor(
                                out=a_[:], in0=w_[:, xc, k:k + 128], scalar=convw[:, xc, k:k + 1], in1=a_[:],
                                op0=ALU.mult, op1=ALU.add), reads=[w_.b, convw.b, a_.b], writes=[a_.b])
                        if xc < SC + 4:
                            dstb, dst = cx_.b, cx_[:, xc, :]
                            P.op("scalar", lambda e, a_=a_, xc=xc, dst=dst: e.activation(
                                out=dst, in_=a_[:], func=AF.Silu, bias=convb[:, xc:xc + 1], scale=1.0),
                                reads=[a_.b, convb.b], writes=[dstb])
                            if xc >= SC:
                                P.op("vector", lambda e, xc=xc, cx_=cx_, bfc_=bfc_: e.tensor_copy(
                                    out=bfc_[:, xc - SC, :], in_=cx_[:, xc, :]), reads=[cx_.b], writes=[bfc_.b])
                        else:
                            P.op("scalar", lambda e, a_=a_, xc=xc, bfc_=bfc_: e.activation(
                                out=bfc_[:, xc - SC, :], in_=a_[:], func=AF.Silu, bias=convb[:, xc:xc + 1], scale=1.0),
                                reads=[a_.b, convb.b], writes=[bfc_.b])
                    xs_ = xs_s[s]
                    for g0 in range(0, SC, 4):
                        n_ = min(4, SC - g0)
                        pt = ps[(g0 // 4) % 2]

                        def tr(e, g0=g0, n_=n_, pt=pt, cx_=cx_):
                            ins = None
                            for j in range(n_):
                                ins = e.transpose(pt[:, j * 128:(j + 1) * 128], cx_[:, g0 + j, :], ident_f)
                            return ins
                        P.op("tensor", tr, reads=[cx_.b, cst.b], writes=[pt.b])
                        P.op("vector", lambda e, g0=g0, n_=n_, pt=pt, xs_=xs_: e.tensor_copy(
                            out=xs_[:, g0 * 128:(g0 + n_) * 128], in_=pt[:, 0:n_ * 128]), reads=[pt.b], writes=[xs_.b])
                    pt = ps[2]

                    def trb(e, pt=pt, cx_=cx_):
                        ins = None
                        for j in range(4):
                            ins = e.transpose(pt[:, j * 128:(j + 1) * 128], cx_[:, SC + j, :], ident_f)
                        return ins
                    P.op("tensor", trb, reads=[cx_.b, cst.b], writes=[pt.b])
                    bt_ = bt_s[s]
                    P.op("vector", lambda e, pt=pt, bt_=bt_: e.tensor_copy(out=bt_[:], in_=pt[:, 0:512]),
                         reads=[pt.b], writes=[bt_.b])
                    P.dma("sync", [(XS[tok0:tok0 + 128, :], xs_[:])], reads=[xs_.b], sem_owner=xs_.b)
                    P.dma("sync", [(BTS[tok0:tok0 + 128, :], bt_[:])], reads=[bt_.b], sem_owner=bt_.b)
                    bfv = BFS.rearrange("g p t -> p g t")
                    cfv = CFS.rearrange("g p t -> p g t")
                    P.dma("sync", [(bfv[:, :, tok0:tok0 + 128], bfc_[:, 0:4, :]), (cfv[:, :, tok0:tok0 + 128], bfc_[:, 4:8, :])],
                          reads=[bfc_.b], sem_owner=bfc_.b)

            for ch in range(NCH_ALL):
                conv_chunk(ch)
            P.barrier()
            P.emit()

        if stage == 3:
            return nc
        phC = contextlib.ExitStack()
        with phC:
            HW = HPG * 64
            hst = [SB(P, "hst%d" % i, [128, 4, HW], F32, stack=phC) for i in range(2)]
            hsb = [SB(P, "hsb%d" % i, [128, 4, HW], BF16, stack=phC) for i in range(2)]
            abc = SB(P, "abc", [128, DT], F32, stack=phC)
            dskb = SB(P, "dskb", [128, SH], F32, stack=phC)
            snormb = SB(P, "snormb", [128, SI], F32, stack=phC)
            xs_l = [SB(P, "xs_l%d" % i, [128, SI], F32, stack=phC) for i in range(2)]
            bt_l = [SB(P, "bt_l%d" % i, [128, 512], BF16, stack=phC) for i in range(2)]
            bc_l = [SB(P, "bc_l%d" % i, [128, 8, 128], BF16, stack=phC) for i in range(2)]
            dt_l = [SB(P, "dt_l%d" % i, [128, DT], F32, stack=phC) for i in range(2)]
            ya_l = [SB(P, "ya_l%d" % i, [128, SI], F32, stack=phC) for i in range(2)]
            zs_l = [SB(P, "zs_l%d" % i, [128, SI], F32, stack=phC) for i in range(2)]
            dA = SB(P, "dA", [128, SH], F32, stack=phC)
            acs = SB(P, "acs", [128, SH], F32, stack=phC)
            dte = SB(P, "dte", [128, SH], F32, stack=phC)
            edec = SB(P, "edec", [128, SH], F32, stack=phC)
            xdt = SB(P, "xdt", [128, SI], F32, stack=phC)
            xdt_b = SB(P, "xdt_b", [128, SI], BF16, stack=phC)
            xdtd_b = SB(P, "xdtd_b", [128, SI], BF16, stack=phC)
            cb = SB(P, "cb", [128, 128], F32, stack=phC)
            rhsA = SB(P, "rhsA", [128, HPG, 128], F32, stack=phC)
            larg = SB(P, "larg", [128, HPG, 128], F32, stack=phC)
            mt = SB(P, "mt", [128, HPG, 128], BF16, stack=phC)
            ec = SB(P, "ec", [128, HPG, 128], F32, stack=phC)
            cs_ = SB(P, "cs_", [128, HPG, 128], BF16, stack=phC)
            yrow = SB(P, "yrow", [128, SI], F32, stack=phC)
            ysq = SB(P, "ysq", [128, SI], F32, stack=phC)
            ssq = SB(P, "ssq", [128, 1], F32, stack=phC)
            rsd = SB(P, "rsd", [128, 1], F32, stack=phC)
            sn_b = SB(P, "sn_b", [128, SI], BF16, stack=phC)
            st_s = [SB(P, "st_s%d" % i, [128, SC, 128], BF16, stack=phC) for i in range(2)]
            tmpd = SB(P, "tmpd", [128, 4, HW], F32, stack=phC)
            eacs = SB(P, "eacs", [128, SH], F32, stack=phC)
            tmpy = SB(P, "tmpy", [128, HW], F32, stack=phC)
            atot = SB(P, "atot", [128, SH], F32, stack=phC)

            P.dma("sync", [(abc[:], alogb_d[:, :])], writes=[abc.b], sem_owner=abc.b)
            P.dma("sync", [(dskb[:], dskb_d[:, :])], writes=[dskb.b], sem_owner=dskb.b)
            P.dma("sync", [(snormb[:], snormb_d[:, :])], writes=[snormb.b], sem_owner=snormb.b)
            P.op("scalar", lambda e: e.activation(out=abc[:], in_=abc[:], func=AF.Exp), reads=[abc.b], writes=[abc.b])
            P.op("vector", lambda e: e.tensor_single_scalar(out=abc[:], in_=abc[:], scalar=-1.0, op=ALU.mult),
                 reads=[abc.b], writes=[abc.b])
            for d_ in range(2):
                P.op("vector", lambda e, d_=d_: e.memset(hst[d_][:], 0.0), writes=[hst[d_].b])
                P.op("vector", lambda e, d_=d_: e.memset(hsb[d_][:], 0.0), writes=[hsb[d_].b])

            bfv = BFS.rearrange("g p t -> p g t")
            cfv = CFS.rearrange("g p t -> p g t")
            nA = (HPG * 128 + 511) // 512
            lctr = [0]

            def ssd_chunk(tok0, d_, need_y, lat_idx):
                s = lctr[0] % 2
                lctr[0] += 1
                xs_, bt_, bc_, dt_ = xs_l[s], bt_l[s], bc_l[s], dt_l[s]
                P.dma("sync", [(xs_[:], XS[tok0:tok0 + 128, :])], writes=[xs_.b], sem_owner=xs_.b)
                P.dma("sync", [(bt_[:], BTS[tok0:tok0 + 128, :])], writes=[bt_.b], sem_owner=bt_.b)
                P.dma("gpsimd", [(bc_[:, 0:4, :], bfv[:, :, tok0:tok0 + 128]), (bc_[:, 4:8, :], cfv[:, :, tok0:tok0 + 128])],
                      writes=[bc_.b], sem_owner=bc_.b)
                P.dma("sync", [(dt_[:], DTA[tok0:tok0 + 128, :])], writes=[dt_.b], sem_owner=dt_.b)
                dts = dt_[:, d_ * SH:(d_ + 1) * SH]
                P.op("vector", lambda e: e.tensor_tensor(out=dA[:], in0=dts, in1=abc[:, d_ * SH:(d_ + 1) * SH], op=ALU.mult),
                     reads=[dt_.b, abc.b], writes=[dA.b])
                pA, pT = ps[0], ps[1]
                P.op("tensor", lambda e: e.matmul(pA[:, 0:SH], lhsT=tri[d_], rhs=dA[:], start=True, stop=True),
                     reads=[dA.b, cst.b], writes=[pA.b])
                P.op("tensor", lambda e: e.matmul(pT[:, 0:SH], lhsT=ones_f[:], rhs=dA[:], start=True, stop=True),
                     reads=[dA.b, ones_f.b], writes=[pT.b])
                P.op("vector", lambda e: e.tensor_copy(out=acs[:], in_=pA[:, 0:SH]), reads=[pA.b], writes=[acs.b])
                P.op("vector", lambda e: e.tensor_copy(out=atot[:], in_=pT[:, 0:SH]), reads=[pT.b], writes=[atot.b])
                P.op("vector", lambda e: e.tensor_tensor(out=dte[:], in0=atot[:], in1=acs[:], op=ALU.subtract),
                     reads=[atot.b, acs.b], writes=[dte.b])
                P.op("scalar", lambda e: e.activation(out=dte[:], in_=dte[:], func=AF.Exp), reads=[dte.b], writes=[dte.b])
                P.op("scalar", lambda e: e.activation(out=edec[:], in_=atot[:], func=AF.Exp), reads=[atot.b], writes=[edec.b])
                x3 = xs_[:].rearrange("p (h d) -> p h d", d=64)
                P.op("vector", lambda e: e.tensor_tensor(
                    out=xdt[:].rearrange("p (h d) -> p h d", d=64), in0=x3,
                    in1=dts.unsqueeze(2).to_broadcast([128, SH, 64]), op=ALU.mult),
                    reads=[xs_.b, dt_.b], writes=[xdt.b])
                P.op("scalar", lambda e: e.activation(out=xdt_b[:], in_=xdt[:], func=AF.Copy), reads=[xdt.b], writes=[xdt_b.b])
                P.op("vector", lambda e: e.tensor_tensor(
                    out=xdtd_b[:].rearrange("p (h d) -> p h d", d=64), in0=xdt[:].rearrange("p (h d) -> p h d", d=64),
                    in1=dte[:].unsqueeze(2).to_broadcast([128, SH, 64]), op=ALU.mult),
                    reads=[xdt.b, dte.b], writes=[xdtd_b.b])
                if need_y and (SSDM & 1):
                    def yop(*a, **k):
                        yc[0] += 1
                        if yc[0] <= YSTEPS:
                            P.op(*a, **k)
                    yc = [0]
                    yop("scalar", lambda e: e.activation(out=eacs[:], in_=acs[:], func=AF.Exp), reads=[acs.b], writes=[eacs.b])
                    for g in range(4):
                        yc = [1]
                        hs = slice(g * HPG, (g + 1) * HPG)
                        pC = ps[2]
                        yop("tensor", lambda e, g=g: e.matmul(pC[:, 0:128], lhsT=bc_[:, g, :], rhs=bc_[:, 4 + g, :],
                                                              start=True, stop=True), reads=[bc_.b], writes=[pC.b])
                        yop("scalar", lambda e: e.activation(out=cb[:], in_=pC[:, 0:128], func=AF.Copy),
                            reads=[pC.b], writes=[cb.b])
                        yop("vector", lambda e, hs=hs: e.tensor_tensor(
                            out=rhsA[:], in0=tri[d_].unsqueeze(1).to_broadcast([128, HPG, 128]),
                            in1=dA[:, hs].unsqueeze(2).to_broadcast([128, HPG, 128]), op=ALU.mult),
                            reads=[dA.b, cst.b], writes=[rhsA.b])
                        pAs = [ps[3], ps[4]][:nA]
                        rflat = rhsA[:].rearrange("p h i -> p (h i)")
                        for bi, pb in enumerate(pAs):
                            w0 = bi * 512
                            w1 = min(HPG * 128, w0 + 512)
                            yop("tensor", lambda e, pb=pb, w0=w0, w1=w1: e.matmul(
                                pb[:, 0:w1 - w0], lhsT=ones_f[:], rhs=rflat[:, w0:w1], start=True, stop=True),
                                reads=[rhsA.b, ones_f.b], writes=[pb.b])
                        for bi, pb in enumerate(pAs):
                            h0 = bi * 4
                            nh_ = min(HPG - h0, 4)
                            for hh in range(nh_):
                                col = g * HPG + h0 + hh
                                yop("vector", lambda e, pb=pb, hh=hh, h0=h0, col=col: e.tensor_scalar(
                                    out=larg[:, h0 + hh, :], in0=pb[:, hh * 128:(hh + 1) * 128], scalar1=acs[:, col:col + 1],
                                    scalar2=None, op0=ALU.subtract), reads=[pb.b, acs.b], writes=[larg.b])
                        yop("vector", lambda e: e.tensor_tensor(
                            out=larg[:], in0=larg[:], in1=negm_[d_].unsqueeze(1).to_broadcast([128, HPG, 128]), op=ALU.add),
                            reads=[larg.b, cst.b], writes=[larg.b])
                        yop("scalar", lambda e: e.activation(out=larg[:], in_=larg[:], func=AF.Exp),
                            reads=[larg.b], writes=[larg.b])
                        yop("vector", lambda e: e.tensor_tensor(
                            out=mt[:], in0=larg[:], in1=cb[:].unsqueeze(1).to_broadcast([128, HPG, 128]), op=ALU.mult),
                            reads=[larg.b, cb.b], writes=[mt.b])
                        pYd, pYo = ps[5], ps[6]

                        def ymm(e, g=g):
                            ins = None
                            for h in range(HPG):
                                hh = g * HPG + h
                                ins = e.matmul(pYd[:, h * 64:(h + 1) * 64], lhsT=mt[:, h, :], rhs=xdt_b[:, hh * 64:(hh + 1) * 64],
                                               start=True, stop=True)
                            return ins
                        yop("tensor", ymm, reads=[mt.b, xdt_b.b], writes=[pYd.b])
                        yop("tensor", lambda e, g=g: e.matmul(pYo[:, 0:HW], lhsT=bc_[:, 4 + g, :], rhs=hsb[d_][:, g, :],
                                                              start=True, stop=True), reads=[bc_.b, hsb[d_].b], writes=[pYo.b])
                        yop("vector", lambda e, g=g: e.tensor_tensor(
                            out=tmpy[:].rearrange("p (h d) -> p h d", d=64),
                            in0=pYo[:, 0:HW].rearrange("p (h d) -> p h d", d=64),
                            in1=eacs[:, g * HPG:(g + 1) * HPG].unsqueeze(2).to_broadcast([128, HPG, 64]), op=ALU.mult),
                            reads=[pYo.b, eacs.b], writes=[tmpy.b])
                        ysl = yrow[:, g * HW:(g + 1) * HW]
                        if d_ == 0:
                            yop("vector", lambda e, g=g, ysl=ysl: e.tensor_tensor(
                                out=ysl.rearrange("p (h d) -> p h d", d=64),
                                in0=xs_[:, g * HW:(g + 1) * HW].rearrange("p (h d) -> p h d", d=64),
                                in1=dskb[:, g * HPG:(g + 1) * HPG].unsqueeze(2).to_broadcast([128, HPG, 64]), op=ALU.mult),
                                reads=[xs_.b, dskb.b], writes=[yrow.b])
                            yop("vector", lambda e, ysl=ysl: e.tensor_tensor(out=ysl, in0=ysl, in1=tmpy[:], op=ALU.add),
                                reads=[tmpy.b, yrow.b], writes=[yrow.b])
                        else:
                            ya_ = ya_l[s]
                            yop("vector", lambda e, g=g, ysl=ysl, ya_=ya_: e.tensor_tensor(
                                out=ysl, in0=ya_[:, g * HW:(g + 1) * HW], in1=tmpy[:], op=ALU.add),
                                reads=[tmpy.b, ya_.b], writes=[yrow.b])
                        yop("vector", lambda e, ysl=ysl: e.tensor_tensor(out=ysl, in0=ysl, in1=pYd[:, 0:HW], op=ALU.add),
                            reads=[pYd.b, yrow.b], writes=[yrow.b])
                for g in range(4):
                    pS = ps[7]
                    P.op("tensor", lambda e, g=g: e.matmul(pS[:, 0:HW], lhsT=bt_[:, g * 128:(g + 1) * 128],
                                                           rhs=xdtd_b[:, g * HW:(g + 1) * HW], start=True, stop=True),
                         reads=[bt_.b, xdtd_b.b], writes=[pS.b])
                    P.op("vector", lambda e, g=g: e.tensor_tensor(
                        out=tmpd[:, g, :].rearrange("p (h d) -> p h d", d=64),
                        in0=hst[d_][:, g, :].rearrange("p (h d) -> p h d", d=64),
                        in1=edec[:, g * HPG:(g + 1) * HPG].unsqueeze(2).to_broadcast([128, HPG, 64]), op=ALU.mult),
                        reads=[hst[d_].b, edec.b, hsb[d_].b], writes=[tmpd.b])
                    P.op("vector", lambda e, g=g: e.tensor_tensor(out=hst[d_][:, g, :], in0=tmpd[:, g, :], in1=pS[:, 0:HW], op=ALU.add),
                         reads=[tmpd.b, pS.b], writes=[hst[d_].b])
                P.op("scalar", lambda e: e.activation(out=hsb[d_][:], in_=hst[d_][:], func=AF.Copy),
                     reads=[hst[d_].b], writes=[hsb[d_].b])
                return s

            ncc = CTX // 128
            for ch in range(ncc):
                ssd_chunk(ch * 128, 0, False, None)
            for ch in reversed(range(ncc)):
                ssd_chunk(ch * 128, 1, False, None)
            for lb in range(NB):
                s = ssd_chunk(CTX + lb * 128, 0, True, lb)
                P.dma("sync", [(YA[lb * 128:(lb + 1) * 128, :], yrow[:])], reads=[yrow.b], sem_owner=yrow.b)
            P.barrier()
            for lb in reversed(range(NB)):
                s = lctr[0] % 2
                ya_, zs_ = ya_l[s], zs_l[s]
                P.dma("sync", [(ya_[:], YA[lb * 128:(lb + 1) * 128, :])], writes=[ya_.b], sem_owner=ya_.b)
                P.dma("sync", [(zs_[:], ZS[lb * 128:(lb + 1) * 128, :])], writes=[zs_.b], sem_owner=zs_.b)
                ssd_chunk(CTX + lb * 128, 1, True, lb)
                if not (SSDM & 2):
                    continue
                P.op("vector", lambda e, zs_=zs_: e.tensor_tensor(out=yrow[:], in0=yrow[:], in1=zs_[:], op=ALU.mult),
                     reads=[yrow.b, zs_.b], writes=[yrow.b])
                P.op("vector", lambda e: e.memset(ssq[:], 0.0), writes=[ssq.b])
                P.op("scalar", lambda e: e.activation(out=ysq[:], in_=yrow[:], func=AF.Square, accum_out=ssq[:]),
                     reads=[yrow.b, ssq.b], writes=[ysq.b, ssq.b])
                P.op("scalar", lambda e: e.activation(out=rsd[:], in_=ssq[:], func=AF.Ln, bias=epsb[:, 0:1], scale=1.0 / SI),
                     reads=[ssq.b, epsb.b], writes=[rsd.b])
                P.op("scalar", lambda e: e.activation(out=rsd[:], in_=rsd[:], func=AF.Exp, scale=-0.5),
                     reads=[rsd.b], writes=[rsd.b])
                P.op("vector", lambda e: e.scalar_tensor_tensor(out=sn_b[:], in0=yrow[:], scalar=rsd[:, 0:1], in1=snormb[:],
                                                                op0=ALU.mult, op1=ALU.mult),
                     reads=[yrow.b, rsd.b, snormb.b], writes=[sn_b.b])
                st_ = st_s[lb % 2]
                for g0 in range(0, SC, 8):
                    n_ = min(8, SC - g0)
                    pt = ps[5 + ((g0 // 8) % 2)]
                    ptb = pt[:].bitcast(BF16)

                    def tr(e, g0=g0, n_=n_, ptb=ptb):
                        ins = None
                        for j in range(n_):
                            ins = e.transpose(ptb[:, j * 128:(j + 1) * 128], sn_b[:, (g0 + j) * 128:(g0 + j + 1) * 128], ident_b[:])
                        return ins
                    P.op("tensor", tr, reads=[sn_b.b, ident_b.b], writes=[pt.b])
                    P.op("vector", lambda e, g0=g0, n_=n_, ptb=ptb, st_=st_: e.tensor_copy(
                        out=st_[:, g0:g0 + n_, :].rearrange("p c t -> p (c t)"), in_=ptb[:, 0:n_ * 128]),
                        reads=[pt.b], writes=[st_.b])
                P.dma("sync", [(MS[lb, :, 16:16 + SC, :], st_[:])], reads=[st_.b], sem_owner=st_.b)
            P.barrier()
            P.emit()

        if stage == 4:
            return nc
        phD = contextlib.ExitStack()
        with phD:
            NKMAX = 384 + CTX
            q_sb = [SB(P, "q_sb%d" % i, [128, 16, 256], BF16, stack=phD) for i in range(2)]
            k_sb = [SB(P, "k_sb%d" % i, [128, 4, 384], BF16, stack=phD) for i in range(2)]
            v_sb = [SB(P, "v_sb%d" % i, [128, 3, 512], BF16, stack=phD) for i in range(2)]
            kc_sb = SB(P, "kc_sb", [128, 4, CTX], BF16, stack=phD)
            vc_sb = SB(P, "vc_sb", [128, CTX // 128, 512], BF16, stack=phD)
            maskt = SB(P, "maskt", [128, 384], F32, stack=phD)
            sinkb = SB(P, "sinkb", [128, 16], F32, stack=phD)
            anormb = SB(P, "anormb", [128, c.AW], F32, stack=phD)
            s_sb = [SB(P, "s_sb%d" % i, [128, NKMAX], F32, stack=phD) for i in range(2)]
            p_sb = [SB(P, "p_sb%d" % i, [128, NKMAX], BF16, stack=phD) for i in range(2)]
            pt_sb = [SB(P, "pt_sb%d" % i, [128, NKMAX], BF16, stack=phD) for i in range(2)]
            sm = [SB(P, "sm%d" % i, [128, 8], F32, stack=phD) for i in range(2)]
            arow = SB(P, "arow", [128, c.AW], F32, stack=phD)
            asq = SB(P, "asq", [128, c.AW], F32, stack=phD)
            an_b = SB(P, "an_b", [128, c.AW], BF16, stack=phD)
            at_s = [SB(P, "at_s%d" % i, [128, 16, 128], BF16, stack=phD) for i in range(2)]
            assq = SB(P, "assq", [128, 2], F32, stack=phD)
            P.dma("sync", [(kc_sb[:], KCT.rearrange("g p t -> p g t"))], writes=[kc_sb.b], sem_owner=kc_sb.b)
            P.dma("sync", [(vc_sb[:], VV[0:CTX, :].rearrange("(b p) c -> p b c", p=128))], writes=[vc_sb.b], sem_owner=vc_sb.b)
            P.dma("sync", [(maskt[:], amask[:, :])], writes=[maskt.b], sem_owner=maskt.b)
            P.dma("sync", [(sinkb[:], sinkb_d[:, :])], writes=[sinkb.b], sem_owner=sinkb.b)
            P.dma("sync", [(anormb[:], anormb_d[:, :])], writes=[anormb.b], sem_owner=anormb.b)
            ktv = KT.rearrange("g p t -> p g t")
            vlv = VV[CTX:CTX + SEQ, :].rearrange("(b p) c -> p b c", p=128)
            scale = float(c.DH) ** -0.5
            def attn_block(qb):
                    s = qb % 2
                    kb0, kb1 = max(qb - 1, 0), min(qb + 1, NB - 1)
                    nkb = kb1 - kb0 + 1
                    nband = nkb * 128
                    m0 = 0 if qb > 0 else 128
                    ntot = nband + CTX
                    nblk = nkb + CTX // 128
                    q_, k_, v_ = q_sb[(qb // 2) % 2], k_sb[s], v_sb[s]
                    qo = (qb % 2) * 128
                    if qb % 2 == 0:
                        P.dma("sync", [(q_[:], QT.rearrange("h p t -> p h t")[:, :, qb * 128:(qb + 2) * 128])], writes=[q_.b], sem_owner=q_.b)
                    P.dma("sync", [(k_[:, :, 0:nband], ktv[:, :, kb0 * 128:(kb1 + 1) * 128])], writes=[k_.b], sem_owner=k_.b)
                    P.dma("sync", [(v_[:, 0:nkb, :], vlv[:, kb0:kb1 + 1, :])], writes=[v_.b], sem_owner=v_.b)
                    def stage1(h):
                        g = h // 4
                        hs_ = h % 2
                        pS, pC = ps[0 + hs_], ps[2 + hs_]
                        ssb, psb_, ptsb, sm_ = s_sb[hs_], p_sb[hs_], pt_sb[hs_], sm[hs_]
                        P.op("tensor", lambda e, h=h, g=g, pS=pS: e.matmul(pS[:, 0:nband], lhsT=q_[:, h, qo:qo + 128], rhs=k_[:, g, 0:nband],
                                                                           start=True, stop=True), reads=[q_.b, k_.b], writes=[pS.b])
                        P.op("tensor", lambda e, h=h, g=g, pC=pC: e.matmul(pC[:, 0:CTX], lhsT=q_[:, h, qo:qo + 128], rhs=kc_sb[:, g, :],
                                                                           start=True, stop=True), reads=[q_.b, kc_sb.b], writes=[pC.b])
                        P.op("vector", lambda e, pS=pS, ssb=ssb: e.scalar_tensor_tensor(
                            out=ssb[:, 0:nband], in0=pS[:, 0:nband], scalar=scale, in1=maskt[:, m0:m0 + nband],
                            op0=ALU.mult, op1=ALU.add), reads=[pS.b, maskt.b], writes=[ssb.b])
                        P.op("scalar", lambda e, pC=pC, ssb=ssb: e.activation(out=ssb[:, nband:ntot], in_=pC[:, 0:CTX], func=AF.Copy, scale=scale),
                             reads=[pC.b], writes=[ssb.b])
                        P.op("vector", lambda e, ssb=ssb, sm_=sm_: e.reduce_max(out=sm_[:, 0:1], in_=ssb[:, 0:ntot], axis=AX.X),
                             reads=[ssb.b], writes=[sm_.b])
                        P.op("vector", lambda e, sm_=sm_, h=h: e.tensor_scalar(out=sm_[:, 1:2], in0=sm_[:, 0:1], scalar1=sinkb[:, h:h + 1],
                                                                              scalar2=-1.0, op0=ALU.max, op1=ALU.mult),
                             reads=[sm_.b, sinkb.b], writes=[sm_.b])
                        P.op("vector", lambda e, sm_=sm_: e.memset(sm_[:, 2:3], 0.0), reads=[sm_.b], writes=[sm_.b])
                        P.op("scalar", lambda e, ssb=ssb, psb_=psb_, sm_=sm_: e.activation(
                            out=psb_[:, 0:ntot], in_=ssb[:, 0:ntot], func=AF.Exp, bias=sm_[:, 1:2], scale=1.0, accum_out=sm_[:, 2:3]),
                            reads=[ssb.b, sm_.b], writes=[psb_.b, sm_.b])
                        P.op("scalar", lambda e, sm_=sm_, h=h: e.activation(out=sm_[:, 3:4], in_=sinkb[:, h:h + 1], func=AF.Exp, bias=sm_[:, 1:2], scale=1.0),
                             reads=[sm_.b, sinkb.b], writes=[sm_.b])

                    def stage2(h):
                        g = h // 4
                        hs_ = h % 2
                        pS, pC = ps[0 + hs_], ps[2 + hs_]
                        ssb, psb_, ptsb, sm_ = s_sb[hs_], p_sb[hs_], pt_sb[hs_], sm[hs_]
                        P.op("vector", lambda e, sm_=sm_: e.tensor_tensor(out=sm_[:, 4:5], in0=sm_[:, 2:3], in1=sm_[:, 3:4], op=ALU.add),
                             reads=[sm_.b], writes=[sm_.b])
                        P.op("vector", lambda e, sm_=sm_: e.reciprocal(out=sm_[:, 5:6], in_=sm_[:, 4:5]), reads=[sm_.b], writes=[sm_.b])
                        pT = ps[4 + hs_]
                        pTb = pT[:].bitcast(BF16)

                        def trp(e, pTb=pTb, psb_=psb_):
                            ins = None
                            for j in range(nblk):
                                ins = e.transpose(pTb[:, j * 128:(j + 1) * 128], psb_[:, j * 128:(j + 1) * 128], ident_b[:])
                            return ins
                        P.op("tensor", trp, reads=[psb_.b, ident_b.b], writes=[pT.b])
                        P.op("scalar", lambda e, pTb=pTb, ptsb=ptsb: e.activation(out=ptsb[:, 0:ntot], in_=pTb[:, 0:ntot], func=AF.Copy),
                             reads=[pT.b], writes=[ptsb.b])
                        pO = ps[6 + hs_]

                        def pv(e, g=g, pO=pO, ptsb=ptsb):
                            ins = None
                            for j in range(nblk):
                                if j < nkb:
                                    vv_ = v_[:, j, g * 128:(g + 1) * 128]
                                else:
                                    vv_ = vc_sb[:, j - nkb, g * 128:(g + 1) * 128]
                                ins = e.matmul(pO[:, 0:128], lhsT=ptsb[:, j * 128:(j + 1) * 128], rhs=vv_,
                                               start=(j == 0), stop=(j == nblk - 1))
                            return ins
                        P.op("tensor", pv, reads=[ptsb.b, v_.b, vc_sb.b], writes=[pO.b])
                        P.op("vector", lambda e, pO=pO, sm_=sm_, h=h: e.tensor_scalar(
                            out=arow[:, h * 128:(h + 1) * 128], in0=pO[:, 0:128], scalar1=sm_[:, 5:6], scalar2=None, op0=ALU.mult),
                            reads=[pO.b, sm_.b], writes=[arow.b])
                    stage1(0)
                    for h in range(1, 16):
                        stage1(h)
                        stage2(h - 1)
                    stage2(15)
                    P.op("vector", lambda e: e.memset(assq[:, 0:1], 0.0), writes=[assq.b])
                    P.op("scalar", lambda e: e.activation(out=asq[:], in_=arow[:], func=AF.Square, accum_out=assq[:, 0:1]),
                         reads=[arow.b, assq.b], writes=[asq.b, assq.b])
                    P.op("scalar", lambda e: e.activation(out=assq[:, 1:2], in_=assq[:, 0:1], func=AF.Ln, bias=epsb[:, 0:1], scale=1.0 / c.AW),
                         reads=[assq.b, epsb.b], writes=[assq.b])
                    P.op("scalar", lambda e: e.activation(out=assq[:, 1:2], in_=assq[:, 1:2], func=AF.Exp, scale=-0.5),
                         reads=[assq.b], writes=[assq.b])
                    P.op("vector", lambda e: e.scalar_tensor_tensor(out=an_b[:], in0=arow[:], scalar=assq[:, 1:2], in1=anormb[:],
                                                                    op0=ALU.mult, op1=ALU.mult),
                         reads=[arow.b, assq.b, anormb.b], writes=[an_b.b])
                    at_ = at_s[s]
                    for g0 in range(0, 16, 8):
                        pt = ps[4 + (g0 // 8)]
                        ptb = pt[:].bitcast(BF16)

                        def tr(e, g0=g0, ptb=ptb):
                            ins = None
                            for j in range(8):
                                ins = e.transpose(ptb[:, j * 128:(j + 1) * 128], an_b[:, (g0 + j) * 128:(g0 + j + 1) * 128], ident_b[:])
                            return ins
                        P.op("tensor", tr, reads=[an_b.b, ident_b.b], writes=[pt.b])
                        P.op("vector", lambda e, g0=g0, ptb=ptb, at_=at_: e.tensor_copy(
                            out=at_[:, g0:g0 + 8, :].rearrange("p c t -> p (c t)"), in_=ptb[:, 0:1024]),
                            reads=[pt.b], writes=[at_.b])
                    P.dma("sync", [(MS[qb, :, 0:16, :], at_[:])], reads=[at_.b], sem_owner=at_.b)

            for qb in range(NB):
                attn_block(qb)
            P.barrier()
            P.emit()

        if stage == 5:
            return nc
        phE = contextlib.ExitStack()
        with phE:
            yacc = SB(P, "yacc2", [128, KC, T], F32, stack=phE)
            ht = SB(P, "ht2", [128, WK, T], BF16, stack=phE)
            mst = ht
            wblk, ring, wdv2 = make_wbufs(phE, "e")
            sgts = [sq[0], SB(P, "sg1e", [128, T], F32, stack=phE)]
            at = [SB(P, "at2%d" % i, [128, FBC, T], BF16, stack=phE) for i in range(2)]
            xgs = [SB(P, "xg2%d" % i, [128, XG, T], F32, stack=phE) for i in range(2)]
            ffn = make_ffn(ht, yacc, ring, wdv2, sgts, at)
            stream = make_stream(xgs)
            wov = w_out.rearrange("(mc p) d -> p mc d", p=128)
            yacc_store2 = Buf("yacc_store2")
            final_evs = []
            BPT = T // 128
            for lt in range(NT):
                tw = T
                x1d = Buf("x1d%d" % lt)
                P.dma("sync", [(mst[:, 0:MC, j * 128:(j + 1) * 128], MS[lt * BPT + j]) for j in range(BPT)],
                      writes=[mst.b], sem_owner=mst.b)
                for b0 in range(0, D, 256):
                    w_ = wblk[(b0 // 256) % 2]
                    load_w(w_, wov, MC, b0, 256)
                    for j in range(2):
                        dc = b0 // 128 + j
                        pb = ps[dc % 2]

                        def mm(e, w_=w_, pb=pb, j=j):
                            ins = None
                            for mc in range(MC):
                                ins = e.matmul(pb[:, 0:tw], lhsT=w_[:, mc, j * 128:(j + 1) * 128], rhs=mst[:, mc, 0:tw],
                                               start=(mc == 0), stop=(mc == MC - 1))
                            return ins
                        P.op("tensor", mm, reads=[w_.b, mst.b], writes=[pb.b])
                        P.op("scalar", lambda e, dc=dc, pb=pb: e.activation(out=yacc[:, dc, 0:tw], in_=pb[:, 0:tw], func=AF.Copy),
                             reads=[pb.b], writes=[yacc.b])
                stats_resident(yacc, tw)
                stream(X1[lt], tw, lambda kc, xg, j: res_chunk(yacc[:, kc, 0:tw], yacc.b, xg[:, j, 0:tw], xg.b,
                                                             yacc[:, kc, 0:tw], yacc.b, kc, 1, 0, tw))
                step = max(1, KC // 2)
                P.dma("sync", [(X1[lt, :, k0:k0 + step, :], yacc[:, k0:k0 + step, :]) for k0 in range(0, KC, step)],
                      reads=[yacc.b], writes=[x1d], sem_owner=yacc_store2)
                stats_resident(yacc, tw)
                for kc in range(KC):
                    mod_chunk(yacc[:, kc, 0:tw], yacc.b, ht, kc, 2, 0, tw)
                ffn(1, tw)
                stats_resident(yacc, tw)

                def fin(kc, xg, j):
                    res_chunk(xg[:, j, 0:tw], xg.b, xg[:, j, 0:tw], xg.b, yacc[:, kc, 0:tw], yacc.b, kc, 2, 0, tw)
                    if j == XG - 1 or kc == KC - 1:
                        g0 = kc - j
                        ev = P.dma("sync", [(out[lt, :, g0:g0 + j + 1, :], xg[:, 0:j + 1, 0:tw])], reads=[xg.b], sem_owner=xg.b)
                        final_evs.append(ev)
                stream(X1[lt], tw, fin, dram_reads=[x1d])
            P.fence_events("sync", final_evs)
            P.barrier()
            P.emit()
    return nc


def feat_major(v):
    return np.ascontiguousarray(v.reshape(-1, 128).T)


def bcast_rows(v):
    return np.ascontiguousarray(np.broadcast_to(v.reshape(1, -1), (128, v.size))).astype(np.float32)


def const_tables(cfg):
    idx = np.arange(128)
    ident = np.eye(128, dtype=np.float32)
    Rm = np.zeros((128, 128), np.float32)
    for d in range(128):
        if (d % 64) < 32:
            Rm[d + 32, d] = -1.0
        else:
            Rm[d - 32, d] = 1.0
    tri_f = (idx[:, None] <= idx[None, :]).astype(np.float32)
    tri_b = (idx[:, None] >= idx[None, :]).astype(np.float32)
    neg_f = np.where(idx[:, None] <= idx[None, :], 0.0, -30000.0).astype(np.float32)
    neg_b = np.where(idx[:, None] >= idx[None, :], 0.0, -30000.0).astype(np.float32)
    consts = np.stack([ident, Rm, tri_f, tri_b, neg_f, neg_b, np.zeros((128, 128), np.float32)], axis=1)
    qi = idx[:, None]
    kj = np.arange(384)[None, :]
    amask = np.where(np.abs(kj - 128 - qi) <= 128, 0.0, -30000.0).astype(np.float32)
    rows = cfg.SEQ // cfg.GRID_W
    row = np.repeat(np.arange(rows), cfg.GRID_W).astype(np.float32)
    col = np.tile(np.arange(cfg.GRID_W), rows).astype(np.float32)
    inv = (np.float32(10000.0) ** (-np.arange(32, dtype=np.float32) / np.float32(32))).astype(np.float32)
    ar = row[:, None] * inv
    ac = col[:, None] * inv
    ang = np.concatenate([ar, ar, ac, ac], axis=-1)
    cosT = np.ascontiguousarray(np.cos(ang).T.astype(np.float32))
    sinT = np.ascontiguousarray(np.sin(ang).T.astype(np.float32))
    return dict(consts=np.ascontiguousarray(consts), amask=amask, cosT=cosT, sinT=sinT)


def make_inputs(cfg, b, tabs, x, c, ctx, c_ctx, w_mod, b_mod, norm_pre, norm_post, w_ffn_gate, w_ffn_up, w_ffn_down,
                w_in, attn_sink, attn_norm, conv_w, conv_b, a_log, dt_bias, d_skip, ssm_norm, w_out):
    KC, T = cfg.KC, cfg.T
    xin_c = np.ascontiguousarray(ctx[b].reshape(cfg.CTX, KC, 128).transpose(2, 1, 0))
    xin_l = np.ascontiguousarray(x[b].reshape(cfg.NT, T, KC, 128).transpose(0, 3, 2, 1))
    cc = np.stack([c[b], c_ctx], axis=-1)
    cT = np.ascontiguousarray(cc.reshape(KC, 128, 2).transpose(1, 0, 2))
    NXC = cfg.XBC // 128
    m = {
        "xin_c": xin_c, "xin_l": xin_l, "cT": cT, "w_mod": w_mod[0], "b_modT": feat_major(b_mod[0]),
        "npreT": feat_major(norm_pre[0].reshape(-1)), "npostT": feat_major(norm_post[0].reshape(-1)),
        "w_in": w_in[0], "w_out": w_out[0],
        "sinkb_d": bcast_rows(attn_sink[0]), "anormb_d": bcast_rows(attn_norm[0]), "snormb_d": bcast_rows(ssm_norm[0]),
        "dtbb_d": bcast_rows(dt_bias[0].reshape(-1)), "alogb_d": bcast_rows(a_log[0].reshape(-1)), "dskb_d": bcast_rows(d_skip[0]),
        "convw_d": np.ascontiguousarray(conv_w[0].reshape(5, NXC, 128).transpose(2, 1, 0)),
        "convb_d": feat_major(conv_b[0]),
    }
    m.update(tabs)
    for i in range(2):
        m["wg%d" % i] = w_ffn_gate[0, i]
        m["wu%d" % i] = w_ffn_up[0, i]
        m["wd%d" % i] = w_ffn_down[0, i]
    return m


def run(cfg, inputs, stage=99, ncores=2, trace=False):
    nc = build(cfg, stage)
    tabs = const_tables(cfg)
    maps = [make_inputs(cfg, b, tabs, **inputs) for b in range(ncores)]
    res = run_bass_kernel_spmd(nc, maps, core_ids=list(range(ncores)), trace=trace)
    outs = []
    for b in range(ncores):
        o = res.results[b]["out"]
        outs.append(o.transpose(0, 3, 2, 1).reshape(cfg.SEQ, cfg.D))
    return np.stack(outs, 0), res


def kernel(**inputs):
    cfg = Cfg()
    inputs = {k: np.asarray(v) for k, v in inputs.items()}
    o, _ = run(cfg, inputs)
    return np.ascontiguousarray(o.astype(np.float32))
```
